# Optimizing a Trainium2 kernel written in Bass

```python
import jax, jax.numpy as jnp
from jax import lax
import numpy as np

D_MODEL = 2048
BATCH = 4
SEQ = 2048
DEPTH = 4

D_MIX = D_MODEL
HEAD_DIM = 128
MLSTM_HEADS = 4
MLSTM_WIDTH = MLSTM_HEADS * HEAD_DIM
MLSTM_CHUNK = 64
CONV_WIDTH = D_MIX // 4
CONV_LEN = 31
NSA_HEADS = 8
NSA_KV_HEADS = 2
NSA_HPG = NSA_HEADS // NSA_KV_HEADS
NSA_WIDTH = NSA_HEADS * HEAD_DIM
NSA_KV_WIDTH = NSA_KV_HEADS * HEAD_DIM
CMP_LEN = 32
CMP_STRIDE = 16
SEL_BLOCK = 64
SEL_COUNT = 16
WINDOW = 512
WIN_Q_BLOCK = 128
SEL_Q_BLOCK = 32
ROPE_THETA = 10000.0
D_FF = 5632
FFN_CONV_LEN = 3
EPS = 1e-6
NEG = -1e30
FORCE = 1e4
IN_SPLITS = (MLSTM_WIDTH, MLSTM_WIDTH, MLSTM_WIDTH, MLSTM_WIDTH, MLSTM_HEADS, MLSTM_HEADS,
             CONV_WIDTH, CONV_WIDTH,
             NSA_WIDTH, NSA_KV_WIDTH, NSA_KV_WIDTH, NSA_KV_WIDTH, NSA_KV_WIDTH, NSA_KV_WIDTH, NSA_KV_WIDTH,
             3 * NSA_HEADS)
N_IN = 4 * MLSTM_WIDTH + 2 * MLSTM_HEADS + 2 * CONV_WIDTH + NSA_WIDTH + 6 * NSA_KV_WIDTH + 3 * NSA_HEADS

kernel_name = "hybrid_mlstm_conformer_nsa_trunk"


def rms_norm(x, w):
    xf = x.astype(jnp.float32)
    y = xf * lax.rsqrt(jnp.mean(xf * xf, axis=-1, keepdims=True) + EPS)
    return (y * w.astype(jnp.float32)).astype(x.dtype)


def layer_norm(x, w, b):
    xf = x.astype(jnp.float32)
    mu = jnp.mean(xf, axis=-1, keepdims=True)
    xc = xf - mu
    y = xc * lax.rsqrt(jnp.mean(xc * xc, axis=-1, keepdims=True) + EPS)
    return (y * w.astype(jnp.float32) + b.astype(jnp.float32)).astype(x.dtype)


def rope(x, pos):
    d = x.shape[-1]
    half = d // 2
    inv = ROPE_THETA ** (-jnp.arange(half, dtype=jnp.float32) / half)
    ang = jnp.asarray(pos).astype(jnp.float32)[:, None] * inv[None, :]
    cos, sin = jnp.cos(ang), jnp.sin(ang)
    xf = x.astype(jnp.float32)
    x1, x2 = xf[..., :half], xf[..., half:]
    return jnp.concatenate([x1 * cos - x2 * sin, x2 * cos + x1 * sin], axis=-1).astype(x.dtype)


def causal_dwconv(x, w):
    k, c = w.shape
    return lax.conv_general_dilated(x, w[:, None, :].astype(x.dtype), window_strides=(1,),
                                    padding=[(k - 1, 0)], dimension_numbers=('NWC', 'WIO', 'NWC'),
                                    feature_group_count=c)


def masked_softmax(s, valid):
    return jax.nn.softmax(jnp.where(valid, s, NEG), axis=-1)


def mlstm_chunkwise(q, k, v, i_pre, f_pre):
    B, H, T, D = q.shape
    L = MLSTM_CHUNK
    NC = T // L
    f32 = jnp.float32

    def chunks(a):
        a = a.astype(f32).reshape((B, H, NC, L) + a.shape[3:])
        return jnp.moveaxis(a, 2, 0)

    logf = jax.nn.log_sigmoid(f_pre.astype(f32))
    causal = jnp.tril(jnp.ones((L, L), dtype=bool))

    def step(carry, inp):
        C, n, m = carry
        qc, kc, vc, ic, lfc = inp
        b = jnp.cumsum(lfc, axis=-1)
        dmat = jnp.where(causal, b[..., :, None] - b[..., None, :] + ic[..., None, :], -jnp.inf)
        inter = b + m[..., None]
        m_t = jnp.maximum(inter, dmat.max(-1))
        s = jnp.einsum('bhtd,bhsd->bhts', qc, kc) * jnp.exp(dmat - m_t[..., None])
        w_inter = jnp.exp(inter - m_t)
        num = jnp.einsum('bhts,bhse->bhte', s, vc) + w_inter[..., None] * jnp.einsum('bhtd,bhde->bhte', qc, C)
        den = s.sum(-1) + w_inter * jnp.einsum('bhtd,bhd->bht', qc, n)
        h = num / jnp.maximum(jnp.abs(den), jnp.exp(-m_t))[..., None]
        b_last = b[..., -1]
        g = b_last[..., None] - b + ic
        m_new = jnp.maximum(b_last + m, g.max(-1))
        w_k = jnp.exp(g - m_new[..., None])
        decay = jnp.exp(b_last + m - m_new)
        C_new = decay[..., None, None] * C + jnp.einsum('bhs,bhsd,bhse->bhde', w_k, kc, vc)
        n_new = decay[..., None] * n + jnp.einsum('bhs,bhsd->bhd', w_k, kc)
        return (C_new, n_new, m_new), h

    init = (jnp.zeros((B, H, D, D), f32), jnp.zeros((B, H, D), f32), jnp.full((B, H), -jnp.inf, f32))
    xs = (chunks(q), chunks(k) * (D ** -0.5), chunks(v), chunks(i_pre), chunks(logf))
    _, h = lax.scan(step, init, xs)
    return jnp.moveaxis(h, 0, 2).reshape(B, H, T, D)


def mlstm_mixer(q, k, v, o, i_pre, f_pre, norm_w):
    B, T, _ = q.shape
    heads = lambda a: a.reshape(B, T, MLSTM_HEADS, HEAD_DIM).transpose(0, 2, 1, 3)
    h = mlstm_chunkwise(heads(q), heads(k), heads(v), i_pre.transpose(0, 2, 1), f_pre.transpose(0, 2, 1))
    h = h.transpose(0, 2, 1, 3)
    mu = jnp.mean(h, axis=-1, keepdims=True)
    hc = h - mu
    h = hc * lax.rsqrt(jnp.mean(hc * hc, axis=-1, keepdims=True) + EPS)
    h = h * norm_w.astype(jnp.float32).reshape(MLSTM_HEADS, HEAD_DIM)
    h = h.reshape(B, T, MLSTM_WIDTH).astype(q.dtype)
    return jax.nn.sigmoid(o) * h


def conformer_conv(a, g, dw_w, dw_b, ln_w, ln_b):
    u = a * jax.nn.sigmoid(g)
    u = causal_dwconv(u, dw_w) + dw_b.astype(u.dtype)
    u = layer_norm(u, ln_w, ln_b)
    return jax.nn.silu(u)


def selected_attention(q, kb, vb, sel):
    B, G, HPG, T, D = q.shape
    K = sel.shape[-1]
    C = SEL_Q_BLOCK
    NQ = T // C
    qs = q.reshape(B, G, HPG, NQ, C, D).transpose(3, 0, 1, 2, 4, 5)
    ss = sel.reshape(B, G, NQ, C, K).transpose(2, 0, 1, 3, 4)
    ps = jnp.arange(T).reshape(NQ, C)
    bi = jnp.arange(B)[:, None, None, None]
    gi = jnp.arange(G)[None, :, None, None]
    offs = jnp.arange(SEL_BLOCK)
    scale = D ** -0.5

    def body(args):
        qc, sc, pc = args
        kg = kb[bi, gi, sc].reshape(B, G, C, K * SEL_BLOCK, D)
        vg = vb[bi, gi, sc].reshape(B, G, C, K * SEL_BLOCK, D)
        kpos = (sc[..., None] * SEL_BLOCK + offs).reshape(B, G, C, K * SEL_BLOCK)
        valid = (kpos <= pc[:, None])[:, :, None]
        s = jnp.einsum('bghcd,bgcnd->bghcn', qc, kg).astype(jnp.float32) * scale
        p = masked_softmax(s, valid)
        return jnp.einsum('bghcn,bgcnd->bghcd', p.astype(vg.dtype), vg)

    out = lax.map(body, (qs, ss, ps))
    return out.transpose(1, 2, 3, 0, 4, 5).reshape(B, G, HPG, T, D)


def window_attention(q, k, v):
    B, G, HPG, T, D = q.shape
    nqb = T // WIN_Q_BLOCK
    kp = jnp.pad(k, ((0, 0), (0, 0), (WINDOW, 0), (0, 0)))
    vp = jnp.pad(v, ((0, 0), (0, 0), (WINDOW, 0), (0, 0)))
    idx = (np.arange(nqb) * WIN_Q_BLOCK)[:, None] + np.arange(WIN_Q_BLOCK + WINDOW)[None, :]
    kb, vb = kp[:, :, idx], vp[:, :, idx]
    qb = q.reshape(B, G, HPG, nqb, WIN_Q_BLOCK, D)
    s = jnp.einsum('bghnqd,bgnkd->bghnqk', qb, kb).astype(jnp.float32) * (D ** -0.5)
    qpos = np.arange(T).reshape(nqb, WIN_Q_BLOCK)[:, :, None]
    kpos = (idx - WINDOW)[:, None, :]
    valid = (kpos >= 0) & (kpos <= qpos) & (qpos - kpos < WINDOW)
    p = masked_softmax(s, jnp.asarray(valid))
    o = jnp.einsum('bghnqk,bgnkd->bghnqd', p.astype(vb.dtype), vb)
    return o.reshape(B, G, HPG, T, D)


def nsa_mixer(q, k_cmp, v_cmp, k_slc, v_slc, k_win, v_win, gates, pe_k, w_k, pe_v, w_v):
    B, T, _ = q.shape
    G, HPG, D = NSA_KV_HEADS, NSA_HPG, HEAD_DIM
    pos = jnp.arange(T)
    q = rope(q.reshape(B, T, G, HPG, D).transpose(0, 2, 3, 1, 4), pos)
    heads_kv = lambda a: a.reshape(B, T, G, D).transpose(0, 2, 1, 3)
    scale = D ** -0.5

    n_cmp = (T - CMP_LEN) // CMP_STRIDE + 1
    cidx = np.arange(n_cmp)[:, None] * CMP_STRIDE + np.arange(CMP_LEN)[None, :]
    cend = np.arange(n_cmp) * CMP_STRIDE + CMP_LEN - 1

    def compress(a, pe, w):
        blk = heads_kv(a)[:, :, cidx] + pe.astype(a.dtype)
        return jnp.einsum('bgnf,fd->bgnd', blk.reshape(B, G, n_cmp, CMP_LEN * D), w.astype(a.dtype))

    kc = rope(compress(k_cmp, pe_k, w_k), cend)
    vc = compress(v_cmp, pe_v, w_v)
    valid_c = jnp.asarray(cend[None, :] <= np.arange(T)[:, None])
    s = jnp.einsum('bghtd,bgnd->bghtn', q, kc).astype(jnp.float32) * scale
    p_cmp = masked_softmax(s, valid_c) * valid_c
    o_cmp = jnp.einsum('bghtn,bgnd->bghtd', p_cmp.astype(vc.dtype), vc)

    n_sel = T // SEL_BLOCK
    k_sel = min(SEL_COUNT, n_sel)
    bstart = np.arange(n_sel) * SEL_BLOCK
    overlap = ((cidx[:, 0][:, None] <= bstart[None, :] + SEL_BLOCK - 1) &
               (cend[:, None] >= bstart[None, :])).astype(np.float32)
    imp = jnp.einsum('bghtn,nj->bgtj', p_cmp, jnp.asarray(overlap))
    j = np.arange(n_sel)[None, :]
    qblk = np.arange(T)[:, None] // SEL_BLOCK
    forced = jnp.asarray((j == 0) | (j == qblk) | (j == qblk - 1))
    valid_s = jnp.asarray(j * SEL_BLOCK <= np.arange(T)[:, None])
    score = jnp.where(forced, FORCE, jnp.where(valid_s, imp, -FORCE))
    _, sel = lax.top_k(score, k_sel)
    ks = rope(heads_kv(k_slc), pos).reshape(B, G, n_sel, SEL_BLOCK, D)
    vs = heads_kv(v_slc).reshape(B, G, n_sel, SEL_BLOCK, D)
    o_slc = selected_attention(q, ks, vs, sel)

    o_win = window_attention(q, rope(heads_kv(k_win), pos), heads_kv(v_win))

    g = jax.nn.sigmoid(gates).reshape(B, T, G, HPG, 3).transpose(0, 2, 3, 1, 4)
    o = g[..., 0:1] * o_cmp + g[..., 1:2] * o_slc + g[..., 2:3] * o_win
    return o.transpose(0, 3, 1, 2, 4).reshape(B, T, NSA_WIDTH)


def setup_inputs(seed: int = 0) -> dict:
    key = jax.random.key(seed)
    ks = jax.random.split(key, 24)
    nrm = lambda k, shape, s: jax.random.normal(k, shape, jnp.float32) * s
    L = DEPTH
    return {
        "x": nrm(ks[0], (BATCH, SEQ, D_MODEL), 1.0),
        "attn_norm_w": 1.0 + nrm(ks[1], (L, D_MODEL), 0.02),
        "w_in": nrm(ks[2], (L, D_MODEL, N_IN), D_MODEL ** -0.5),
        "mlstm_i_bias": nrm(ks[3], (L, MLSTM_HEADS), 0.1),
        "mlstm_f_bias": jnp.linspace(3.0, 6.0, MLSTM_HEADS, dtype=jnp.float32)[None, :] + nrm(ks[4], (L, MLSTM_HEADS), 0.1),
        "mlstm_norm_w": 1.0 + nrm(ks[5], (L, MLSTM_WIDTH), 0.02),
        "conv_dw_w": nrm(ks[6], (L, CONV_LEN, CONV_WIDTH), CONV_LEN ** -0.5),
        "conv_dw_b": nrm(ks[7], (L, CONV_WIDTH), 0.02),
        "conv_ln_w": 1.0 + nrm(ks[8], (L, CONV_WIDTH), 0.02),
        "conv_ln_b": nrm(ks[9], (L, CONV_WIDTH), 0.02),
        "nsa_cmp_pe_k": nrm(ks[10], (L, CMP_LEN, HEAD_DIM), 0.02),
        "nsa_cmp_w_k": nrm(ks[11], (L, CMP_LEN * HEAD_DIM, HEAD_DIM), (CMP_LEN * HEAD_DIM) ** -0.5),
        "nsa_cmp_pe_v": nrm(ks[12], (L, CMP_LEN, HEAD_DIM), 0.02),
        "nsa_cmp_w_v": nrm(ks[13], (L, CMP_LEN * HEAD_DIM, HEAD_DIM), (CMP_LEN * HEAD_DIM) ** -0.5),
        "w_out": nrm(ks[14], (L, D_MIX, D_MODEL), (D_MIX * 2 * DEPTH) ** -0.5),
        "ffn_norm_w": 1.0 + nrm(ks[15], (L, D_MODEL), 0.02),
        "w_up": nrm(ks[16], (L, D_MODEL, 2 * D_FF), D_MODEL ** -0.5),
        "ffn_dw_w": nrm(ks[17], (L, FFN_CONV_LEN, 2 * D_FF), FFN_CONV_LEN ** -0.5),
        "w_down": nrm(ks[18], (L, D_FF, D_MODEL), (D_FF * 2 * DEPTH) ** -0.5),
        "final_norm_w": 1.0 + nrm(ks[19], (D_MODEL,), 0.02),
    }


def reference(x, attn_norm_w, w_in, mlstm_i_bias, mlstm_f_bias, mlstm_norm_w, conv_dw_w, conv_dw_b,
              conv_ln_w, conv_ln_b, nsa_cmp_pe_k, nsa_cmp_w_k, nsa_cmp_pe_v, nsa_cmp_w_v, w_out,
              ffn_norm_w, w_up, ffn_dw_w, w_down, final_norm_w):
    split_at = list(np.cumsum(IN_SPLITS)[:-1])
    for l in range(DEPTH):
        h = rms_norm(x, attn_norm_w[l])
        proj = jnp.einsum('btd,dn->btn', h, w_in[l])
        (mq, mk, mv, mo, mi, mf, ca, cg, nq, nkc, nvc, nks, nvs, nkw, nvw, ngt) = jnp.split(proj, split_at, axis=-1)
        y_a = mlstm_mixer(mq, mk, mv, mo, mi + mlstm_i_bias[l], mf + mlstm_f_bias[l], mlstm_norm_w[l])
        y_b = conformer_conv(ca, cg, conv_dw_w[l], conv_dw_b[l], conv_ln_w[l], conv_ln_b[l])
        y_c = nsa_mixer(nq, nkc, nvc, nks, nvs, nkw, nvw, ngt,
                        nsa_cmp_pe_k[l], nsa_cmp_w_k[l], nsa_cmp_pe_v[l], nsa_cmp_w_v[l])
        y = jnp.concatenate([y_a, y_b, y_c], axis=-1)
        x = x + jnp.einsum('btm,md->btd', y, w_out[l])
        h = rms_norm(x, ffn_norm_w[l])
        u = causal_dwconv(jnp.einsum('btd,df->btf', h, w_up[l]), ffn_dw_w[l])
        gate, val = jnp.split(u, 2, axis=-1)
        x = x + jnp.einsum('btf,fd->btd', jax.nn.silu(gate) * val, w_down[l])
    return rms_norm(x, final_norm_w)
```

```python
import numpy as np
from contextlib import ExitStack
import concourse.bass as bass
import concourse.mybir as mybir
from concourse.bass_utils import run_bass_kernel_spmd

F32 = mybir.dt.float32
BF16 = mybir.dt.bfloat16
AF = mybir.ActivationFunctionType
ALU = mybir.AluOpType
AX = mybir.AxisListType

T = 2048
D = 2048
L = 4
NIN = 5664
DFF = 5632
EPS = 1e-6
NEGM = -30000.0
EPOCH = 30000


class Op:
    __slots__ = ("eng", "fn", "reads", "writes", "dma", "deps", "sig", "idx")

    def __init__(self, eng, fn, reads, writes, dma):
        self.eng, self.fn, self.reads, self.writes, self.dma = eng, fn, reads, writes, dma
        self.deps = ()
        self.sig = None


class Prog:
    ENGS = ("pe", "act", "dve", "pool", "sp")

    def __init__(self, nc, n_dma_sems=16):
        self.nc = nc
        self.ops = []
        self.n_dma_sems = n_dma_sems

    def _add(self, eng, fn, reads, writes, dma):
        reads, writes = list(reads), list(writes)
        for r in reads:
            if isinstance(r, tuple) and r and r[0] == "ps" and r not in writes:
                writes.append(r)
        o = Op(eng, fn, tuple(reads), tuple(writes), dma)
        o.idx = len(self.ops)
        self.ops.append(o)
        return o

    def pe(self, fn, reads=(), writes=()):
        return self._add("pe", fn, reads, writes, False)

    def act(self, fn, reads=(), writes=()):
        return self._add("act", fn, reads, writes, False)

    def dve(self, fn, reads=(), writes=()):
        return self._add("dve", fn, reads, writes, False)

    def pool(self, fn, reads=(), writes=()):
        return self._add("pool", fn, reads, writes, False)

    def dma(self, q, fn, reads=(), writes=()):
        return self._add(q, fn, reads, writes, True)

    def barrier(self):
        return self._add(None, None, (), (), False)

    def emit(self, stack):
        nc = self.nc
        ops = self.ops
        engs = {"pe": nc.tensor, "act": nc.scalar, "dve": nc.vector, "pool": nc.gpsimd, "sp": nc.sync}
        last_w, readers, needed = {}, {}, set()
        last_compute = {}
        dmas_since = []
        pending = {e: set() for e in engs}
        for o in ops:
            if o.eng is None:
                bd = set(last_compute.values()) | set(dmas_since)
                for e in engs:
                    pending[e] |= bd
                dmas_since = []
                continue
            deps = set()
            for r in o.reads:
                j = last_w.get(r)
                if j is not None:
                    deps.add(j)
            for w in o.writes:
                j = last_w.get(w)
                if j is not None:
                    deps.add(j)
                deps.update(readers.get(w, ()))
            if pending[o.eng]:
                deps |= pending[o.eng]
                pending[o.eng] = set()
            deps.discard(o.idx)
            fd = []
            for j in deps:
                p = ops[j]
                if p.eng == o.eng == "pe":
                    continue
                fd.append(j)
                needed.add(j)
            o.deps = fd
            for w in o.writes:
                last_w[w] = o.idx
                readers[w] = []
            for r in o.reads:
                if r not in o.writes:
                    readers.setdefault(r, []).append(o.idx)
            if o.dma:
                dmas_since.append(o.idx)
            else:
                last_compute[o.eng] = o.idx
        cnt = {e: 0 for e in engs}
        eng_sems = {e: [] for e in engs}
        dma_sems, dma_rr, dma_uses = {}, {}, {}

        def get_eng_sem(e, ep):
            while len(eng_sems[e]) <= ep:
                eng_sems[e].append(stack.enter_context(nc.semaphore(f"c_{e}_{len(eng_sems[e])}")))
            return eng_sems[e][ep]

        reuse_wait = {}
        for o in ops:
            if o.eng is None:
                continue
            if o.dma:
                q = o.eng
                if q not in dma_sems:
                    dma_sems[q] = [stack.enter_context(nc.semaphore(f"d_{q}_{k}")) for k in range(self.n_dma_sems)]
                    dma_rr[q] = 0
                k = dma_rr[q]
                dma_rr[q] = (k + 1) % self.n_dma_sems
                u = dma_uses.get((q, k), 0) + 1
                dma_uses[(q, k)] = u
                o.sig = (dma_sems[q][k], 16 * u, 16)
                if u > 1:
                    reuse_wait[o.idx] = (dma_sems[q][k], 16 * (u - 1))
            elif o.idx in needed:
                c = cnt[o.eng]
                cnt[o.eng] = c + 1
                o.sig = (get_eng_sem(o.eng, c // EPOCH), c % EPOCH + 1, 1)
        waited = {e: {} for e in engs}
        nwaits = 0
        for o in ops:
            if o.eng is None:
                continue
            eng = engs[o.eng]
            wl = {}
            for j in o.deps:
                s, v, _ = ops[j].sig
                if wl.get(s.num, (None, 0))[1] < v:
                    wl[s.num] = (s, v)
            if o.idx in reuse_wait:
                s, v = reuse_wait[o.idx]
                if wl.get(s.num, (None, 0))[1] < v:
                    wl[s.num] = (s, v)
            for num, (s, v) in wl.items():
                if waited[o.eng].get(num, 0) >= v:
                    continue
                eng.wait_ge(s, v)
                waited[o.eng][num] = v
                nwaits += 1
            ins = o.fn(eng)
            if o.sig is not None:
                s, v, inc = o.sig
                ins.then_inc(s, inc)
        for q, sems in dma_sems.items():
            for k, s in enumerate(sems):
                u = dma_uses.get((q, k), 0)
                if u:
                    nc.sync.wait_ge(s, 16 * u)
        for e, sems in eng_sems.items():
            c = cnt[e]
            for ep, s in enumerate(sems):
                v = min(EPOCH, c - ep * EPOCH)
                if v > 0:
                    nc.sync.wait_ge(s, v)
        self.stats = dict(n_ops=len(ops), n_waits=nwaits, sig=dict(cnt))


_ORIG = dict(mq=(0, 512), mk=(512, 1024), mv=(1024, 1536), mo=(1536, 2048), mi=(2048, 2052), mf=(2052, 2056),
             ca=(2056, 2568), cg=(2568, 3080), nq=(3080, 4104), nkc=(4104, 4360), nvc=(4360, 4616),
             nks=(4616, 4872), nvs=(4872, 5128), nkw=(5128, 5384), nvw=(5384, 5640), ngt=(5640, 5664))
_ORDER = ["mq", "mk", "mv", "mo", "ca", "cg", "nq", "nkc", "nvc", "nks", "nkw", "nvs", "nvw", "mi", "mf", "ngt"]
PERM = np.concatenate([np.arange(*_ORIG[k]) for k in _ORDER])


def host_consts():
    import ml_dtypes
    bf = ml_dtypes.bfloat16
    c = {}
    half = 64
    inv = (np.float32(10000.0) ** (-np.arange(half, dtype=np.float32) / np.float32(half))).astype(np.float32)

    def tabs(pos):
        ang = pos.astype(np.float32)[:, None] * inv[None, :]
        cos, sin = np.cos(ang).astype(np.float32), np.sin(ang).astype(np.float32)
        cosT = np.concatenate([cos, cos], 1).T
        sinT = np.concatenate([-sin, sin], 1).T
        return np.ascontiguousarray(cosT), np.ascontiguousarray(sinT)

    c["cosT"], c["sinT"] = tabs(np.arange(T))
    cend = np.arange(127) * 16 + 31
    cc, ss = tabs(cend)
    c["cosC"] = np.zeros((128, 128), np.float32); c["cosC"][:, :127] = cc
    c["sinC"] = np.zeros((128, 128), np.float32); c["sinC"][:, :127] = ss
    ident = np.eye(128, dtype=np.float32)
    c["ident"] = ident.astype(bf)
    pswap = np.zeros((128, 128), np.float32)
    for d in range(64):
        pswap[d, d + 64] = 1.0
        pswap[d + 64, d] = 1.0
    c["pswap"] = pswap.astype(bf)
    a = np.arange(128)
    tri = np.where(a[:, None] <= a[None, :], 0.0, NEGM).astype(np.float32)
    far = np.where(a[:, None] > a[None, :], 0.0, NEGM).astype(np.float32)
    c["trineg"] = np.tile(tri, (1, 4)).astype(bf)
    c["farneg"] = np.tile(far, (1, 4)).astype(bf)
    c["tri01"] = np.tile((a[:, None] <= a[None, :]).astype(np.float32), (1, 4))
    c["triU"] = (a[:, None] <= a[None, :]).astype(np.float32)
    n = np.arange(128)
    cm = np.where((n[:, None] * 16 + 31 <= np.arange(T)[None, :]) & (n[:, None] < 127), 0.0, NEGM).astype(np.float32)
    c["cmpneg"] = cm.astype(bf)
    nn = np.arange(127)
    bstart = np.arange(32) * 64
    ov = ((nn[:, None] * 16 <= bstart[None, :] + 63) & (nn[:, None] * 16 + 31 >= bstart[None, :])).astype(np.float32)
    ovp = np.zeros((128, 32), np.float32); ovp[:127] = ov
    c["ovl"] = ovp.astype(bf)
    key = np.arange(T)
    c["expand"] = (key[None, :] // 64 == np.arange(32)[:, None]).astype(np.float32).astype(bf)
    t = np.arange(T)
    j = np.arange(32)[None, :]
    qblk = t[:, None] // 64
    forced = (j == 0) | (j == qblk) | (j == qblk - 1)
    valid = j * 64 <= t[:, None]
    vm = (valid & ~forced).astype(np.float32)
    am = np.where(forced, 1e4, np.where(valid, 0.0, -1e4)).astype(np.float32)
    c["selvm"] = np.ascontiguousarray(vm.reshape(16, 128, 32).transpose(1, 0, 2))
    c["selam"] = np.ascontiguousarray(am.reshape(16, 128, 32).transpose(1, 0, 2))
    return c


CONST_SPECS = dict(cosT=([128, T], F32), sinT=([128, T], F32), cosC=([128, 128], F32), sinC=([128, 128], F32),
                   ident=([128, 128], BF16), pswap=([128, 128], BF16), trineg=([128, 512], BF16),
                   farneg=([128, 512], BF16), tri01=([128, 512], F32), triU=([128, 128], F32),
                   cmpneg=([128, T], BF16), ovl=([128, 32], BF16), expand=([32, T], BF16),
                   selvm=([128, 16, 32], F32), selam=([128, 16, 32], F32))


class K:
    pass


_UID = [0]


def _uniq(name):
    _UID[0] += 1
    return f"{name}_u{_UID[0]}"


def build(n_layers=L, stages=("s1", "conf", "mlstm", "nsa", "wout", "ffn"), debug=(), LW=L):
    nc = bass.Bass("TRN2", target_bir_lowering=False)
    k = K()
    k.nc = nc
    k.debug = debug
    k.stages = stages

    def din(name, shape, dt):
        return nc.dram_tensor(name, list(shape), dt, kind="ExternalInput").ap()

    def dscr(name, shape, dt):
        kind = "ExternalOutput" if name in debug else "Internal"
        return nc.dram_tensor(name, list(shape), dt, kind=kind).ap()

    k.xT_in = din("xT", [D, T], F32)
    k.w_in = din("w_in", [LW, D, NIN], F32)
    k.w_out = din("w_out", [LW, D, D], F32)
    k.w_up = din("w_up", [LW, D, 2 * DFF], F32)
    k.w_down = din("w_down", [LW, DFF, D], F32)
    k.nw1 = din("nw1", [L, 128, 16], F32)
    k.nw2 = din("nw2", [L, 128, 16], F32)
    k.nwf = din("nwf", [128, 16], F32)
    k.gbias = din("gbias", [L, 8], F32)
    k.mnw = din("mnw", [L, 512], F32)
    k.cdw = din("cdw", [L, 4, 128, 31], F32)
    k.cvec = din("cvec", [L, 4, 128, 3], F32)
    k.peT = din("peT", [L, 2, 128, 32], F32)
    k.cmpw = din("cmpw", [L, 2, 4096, 128], F32)
    k.fdw = din("fdw", [L, 128, 88, 3], F32)
    k.consts = {n: din("c_" + n, s, dt) for n, (s, dt) in CONST_SPECS.items()}
    k.outT = nc.dram_tensor("outT", [D, T], F32, kind="ExternalOutput").ap()

    k.mqT = dscr("mqT", [512, T], BF16)
    k.mkT = dscr("mkT", [512, T], BF16)
    k.mk_tm = dscr("mk_tm", [T, 512], BF16)
    k.mv_tm = dscr("mv_tm", [T, 512], BF16)
    k.mo_tm = dscr("mo_tm", [T, 512], BF16)
    k.nqT = dscr("nqT", [2, 128, 16, 4, 128], BF16)
    k.ncT = dscr("ncT", [512, T], BF16)
    k.nkT = dscr("nkT", [512, T], BF16)
    k.nv_tm = dscr("nv_tm", [T, 512], BF16)
    k.gates = dscr("gates", [T, 32], F32)
    k.yT = dscr("yT", [D, T], BF16)
    k.xm = dscr("xm", [D, T], F32)
    k.xo = dscr("xo", [D, T], F32)
    k.aT = dscr("aT", [DFF, T], BF16)

    with ExitStack() as st:
        P = Prog(nc)
        k.P = P
        k.ps = st.enter_context(nc.psum_tensor("ps", [128, 8, 512], F32))
        sb = lambda name, shape, dt: st.enter_context(nc.sbuf_tensor(_uniq(name), list(shape), dt))
        k.ident = sb("ident", [128, 128], BF16)
        k.pswap = sb("pswap", [128, 128], BF16)
        k.ones_bf = sb("ones_bf", [128, 128], BF16)
        k.ones_f = sb("ones_f", [128, 128], F32)
        P.dma("sp", lambda e: e.dma_start(out=k.ident[:], in_=k.consts["ident"]), writes=["ident"])
        P.dma("sp", lambda e: e.dma_start(out=k.pswap[:], in_=k.consts["pswap"]), writes=["pswap"])
        P.dve(lambda e: e.memset(k.ones_bf[:], 1.0), writes=["ones_bf"])
        P.dve(lambda e: e.memset(k.ones_f[:], 1.0), writes=["ones_f"])
        xcur, xkey = k.xT_in, "xin"
        for l in range(n_layers):
            if "s1" in stages:
                stage_s1(k, l, xcur, xkey)
            P.barrier()
            if "mlstm" in stages:
                stage_mlstm(k, l)
            if "nsa" in stages:
                stage_nsa(k, l)
            if "wout" in stages:
                stage_wout(k, l, xcur, xkey)
            if "ffn" in stages:
                stage_ffn(k, l)
                xcur, xkey = k.xo, "xo"
        if "final" in stages:
            stage_final(k)
        P.emit(st)
        k.stats = P.stats
    return nc, k


def ps_keys(b0, n=1):
    return [("ps", b) for b in range(b0, b0 + n)]


def rms_to_hT(k, st, l, x_src, xkey, nw_dram, hT):
    nc, P, ps = k.nc, k.P, k.ps
    sb = lambda name, shape, dt: st.enter_context(nc.sbuf_tensor(_uniq(name), list(shape), dt))
    xin = [sb(f"rn_xin{i}", [128, T], F32) for i in range(2)]
    sq = [sb(f"rn_sq{i}", [128, T], BF16) for i in range(2)]
    rstd = sb("rn_rstd", [128, T], F32)
    nw = sb("rn_nw", [128, 16], F32)
    P.dma("sp", lambda e: e.dma_start(out=nw[:], in_=nw_dram), writes=["rn_nw"])
    for kc in range(16):
        i = kc % 2
        P.dma("sp", lambda e, kc=kc, i=i: e.dma_start(out=xin[i][:], in_=x_src[kc * 128:(kc + 1) * 128, :]),
              reads=[(xkey, kc)], writes=[("rn_xin", i)])
        P.act(lambda e, i=i: e.activation(out=sq[i][:], in_=xin[i][:], func=AF.Square),
              reads=[("rn_xin", i)], writes=[("rn_sq", i)])
        for q in range(4):
            P.pe(lambda e, kc=kc, i=i, q=q: e.matmul(ps[:, q, :], k.ones_bf[:], sq[i][:, q * 512:(q + 1) * 512],
                                                      start=(kc == 0), stop=(kc == 15)),
                 reads=[("rn_sq", i), "ones_bf"], writes=ps_keys(q))
    for q in range(4):
        P.act(lambda e, q=q: e.activation(out=rstd[:, q * 512:(q + 1) * 512], in_=ps[:, q, :], func=AF.Sqrt,
                                          bias=EPS, scale=1.0 / D),
              reads=ps_keys(q), writes=["rn_rstd"])
    P.dve(lambda e: e.reciprocal(out=rstd[:], in_=rstd[:]), reads=["rn_rstd"], writes=["rn_rstd"])
    for kc in range(16):
        i = kc % 2
        P.dma("sp", lambda e, kc=kc, i=i: e.dma_start(out=xin[i][:], in_=x_src[kc * 128:(kc + 1) * 128, :]),
              reads=[(xkey, kc)], writes=[("rn_xin", i)])
        P.dve(lambda e, kc=kc, i=i: e.scalar_tensor_tensor(out=hT[:, kc, :], in0=xin[i][:], scalar=nw[:, kc:kc + 1],
                                                           in1=rstd[:], op0=ALU.mult, op1=ALU.mult),
              reads=[("rn_xin", i), "rn_rstd", "rn_nw"], writes=["hT"])


def stage_s1(k, l, x_src, xkey):
    nc, P, ps = k.nc, k.P, k.ps
    with ExitStack() as st:
        sb = lambda name, shape, dt: st.enter_context(nc.sbuf_tensor(_uniq(name), list(shape), dt))
        hT = sb("hT", [128, 16, T], BF16)
        with ExitStack() as st2:
            rms_to_hT(k, st2, l, x_src, xkey, k.nw1[l], hT)
        P.barrier()
        wb = [sb(f"wb{i}", [128, 16, 512], BF16) for i in range(2)]
        st_in = ExitStack()
        sb_outer = sb
        sb = lambda name, shape, dt: st_in.enter_context(nc.sbuf_tensor(_uniq(name), list(shape), dt))
        cosT = sb("cosT", [128, T], F32)
        sinT = sb("sinT", [128, T], F32)
        P.dma("sp", lambda e: e.dma_start(out=cosT[:], in_=k.consts["cosT"]), writes=["cosT"])
        P.dma("sp", lambda e: e.dma_start(out=sinT[:], in_=k.consts["sinT"]), writes=["sinT"])
        ob = [sb(f"ob{i}", [128, 1024], BF16) for i in range(4)]
        qb = [sb(f"qb{i}", [128, 1024], BF16) for i in range(2)]
        t1 = [sb(f"t1_{i}", [128, 1024], F32) for i in range(2)]
        t2 = [sb(f"t2_{i}", [128, 1024], F32) for i in range(2)]
        og = [sb(f"og{i}", [128, 32], F32) for i in range(2)]
        wv = k.w_in[l].rearrange("(kc p) n -> p kc n", p=128)
        groups = [
            (0, 512, "fm", k.mqT), (512, 512, "fm+tm", (k.mkT, k.mk_tm)), (1024, 512, "tm", k.mv_tm),
            (1536, 512, "tm", k.mo_tm), (3072, 512, "ropeq", 0), (3584, 512, "ropeq", 1),
            (4096, 512, "fm", k.ncT), (4608, 512, "rope", k.nkT), (5120, 512, "tm", k.nv_tm),
            (5632, 32, "gates", k.gates),
        ]
        cnt = dict(ob=0, rope=0, fm=0, tm=0, og=0)
        deferred = []

        def flush_deferred():
            while deferred:
                deferred.pop(0)()

        def fm_chunk(gi, c, kind, dest):
            for half in range(2):
                b0 = (cnt["fm"] % 3) * 2
                cnt["fm"] += 1
                for kc in range(16):
                    for q in range(2):
                        P.pe(lambda e, kc=kc, q=q, b0=b0, half=half: e.matmul(
                            ps[:, b0 + q, :], wb[gi % 2][:, kc, c * 128:(c + 1) * 128],
                            hT[:, kc, half * 1024 + q * 512: half * 1024 + (q + 1) * 512],
                            start=(kc == 0), stop=(kc == 15)),
                            reads=[("wb", gi % 2), "hT"], writes=[("ps", b0 + q)])
                flush_deferred()
                psv = ps[:, b0:b0 + 2, :].rearrange("p a b -> p (a b)")
                cols = slice(half * 1024, (half + 1) * 1024)
                oi = cnt["ob"] % 4
                cnt["ob"] += 1
                if kind == "fm":
                    P.act(lambda e, psv=psv, oi=oi: e.activation(out=ob[oi][:], in_=psv, func=AF.Copy),
                          reads=ps_keys(b0, 2), writes=[("ob", oi)])
                    P.dma("sp", lambda e, oi=oi, cols=cols: e.dma_start(out=dest[c * 128:(c + 1) * 128, cols], in_=ob[oi][:]),
                          reads=[("ob", oi)], writes=[("scr", id(dest), c, half)])
                else:
                    ri = cnt["rope"] % 2
                    cnt["rope"] += 1
                    P.act(lambda e, psv=psv, ri=ri: e.activation(out=qb[ri][:], in_=psv, func=AF.Copy),
                          reads=ps_keys(b0, 2), writes=[("qb", ri)])
                    for q in range(2):
                        P.dve(lambda e, q=q, b0=b0, ri=ri, half=half: e.tensor_tensor(
                            out=t1[ri][:, q * 512:(q + 1) * 512], in0=ps[:, b0 + q, :],
                            in1=cosT[:, half * 1024 + q * 512: half * 1024 + (q + 1) * 512], op=ALU.mult),
                            reads=ps_keys(b0 + q) + ["cosT", ("qb", ri)], writes=[("t1", ri)])

                    def rot(ri=ri, cols=cols, oi=oi, half=half):
                        for q in range(2):
                            P.pe(lambda e, q=q: e.matmul(ps[:, 6 + q, :], k.pswap[:], qb[ri][:, q * 512:(q + 1) * 512],
                                                         start=True, stop=True),
                                 reads=[("qb", ri), "pswap"], writes=[("ps", 6 + q)])
                        for q in range(2):
                            P.dve(lambda e, q=q: e.tensor_tensor(
                                out=t2[ri][:, q * 512:(q + 1) * 512], in0=ps[:, 6 + q, :],
                                in1=sinT[:, half * 1024 + q * 512: half * 1024 + (q + 1) * 512], op=ALU.mult),
                                reads=ps_keys(6 + q) + ["sinT"], writes=[("t2", ri)])
                        P.dve(lambda e: e.tensor_tensor(out=ob[oi][:], in0=t1[ri][:], in1=t2[ri][:], op=ALU.add),
                               reads=[("t1", ri), ("t2", ri)], writes=[("ob", oi)])
                        if kind == "ropeq":
                            g = dest
                            dv = k.nqT[g, :, half * 8:(half + 1) * 8, c, :]
                            P.dma("sp", lambda e: e.dma_start(out=dv, in_=ob[oi][:].rearrange("p (a b) -> p a b", b=128)),
                                  reads=[("ob", oi)], writes=[("scr", "nqT", g, c, half)])
                        else:
                            P.dma("sp", lambda e: e.dma_start(out=dest[c * 128:(c + 1) * 128, cols], in_=ob[oi][:]),
                                  reads=[("ob", oi)], writes=[("scr", id(dest), c, half)])
                    deferred.append(rot)

        def tm_group(gi, ncols, dest, gates=False):
            for tt in range(16):
                b0 = cnt["tm"] % 6
                cnt["tm"] += 1
                for kc in range(16):
                    P.pe(lambda e, kc=kc, b0=b0, tt=tt: e.matmul(
                        ps[:, b0, 0:ncols], hT[:, kc, tt * 128:(tt + 1) * 128], wb[gi % 2][:, kc, 0:ncols],
                        start=(kc == 0), stop=(kc == 15)),
                        reads=[("wb", gi % 2), "hT"], writes=[("ps", b0)])
                flush_deferred()
                if gates:
                    oi = cnt["og"] % 2
                    cnt["og"] += 1
                    P.act(lambda e, b0=b0, oi=oi: e.activation(out=og[oi][:], in_=ps[:, b0, 0:32], func=AF.Copy),
                          reads=ps_keys(b0), writes=[("og", oi)])
                    P.dma("sp", lambda e, oi=oi, tt=tt: e.dma_start(out=dest[tt * 128:(tt + 1) * 128, :], in_=og[oi][:]),
                          reads=[("og", oi)], writes=[("scr", "gates", tt)])
                else:
                    oi = cnt["ob"] % 4
                    cnt["ob"] += 1
                    P.act(lambda e, b0=b0, oi=oi: e.activation(out=ob[oi][:, 0:512], in_=ps[:, b0, :], func=AF.Copy),
                          reads=ps_keys(b0), writes=[("ob", oi)])
                    P.dma("sp", lambda e, oi=oi, tt=tt: e.dma_start(out=dest[tt * 128:(tt + 1) * 128, :], in_=ob[oi][:, 0:512]),
                          reads=[("ob", oi)], writes=[("scr", id(dest), tt)])

        import os
        _bis = os.environ.get("BISECT")
        if _bis is not None:
            groups = [groups[int(x)] for x in _bis.split(",") if x != ""]
        for gi, (c0, ncols, kind, dest) in enumerate(groups):
            P.dma("pool", lambda e, gi=gi, c0=c0, ncols=ncols: e.dma_start(out=wb[gi % 2][:, :, 0:ncols], in_=wv[:, :, c0:c0 + ncols]),
                  writes=[("wb", gi % 2)])
            if kind == "fm":
                for c in range(4):
                    fm_chunk(gi, c, "fm", dest)
            elif kind == "fm+tm":
                for c in range(4):
                    fm_chunk(gi, c, "fm", dest[0])
                tm_group(gi, 512, dest[1])
            elif kind == "tm":
                tm_group(gi, 512, dest)
            elif kind in ("ropeq", "rope"):
                for c in range(4):
                    fm_chunk(gi, c, kind, dest)
            elif kind == "gates":
                tm_group(gi, 32, dest, gates=True)
        flush_deferred()
        P.barrier()
        st_in.close()
        if "conf" in k.stages:
            stage_conf(k, l, hT, wb)
    P.barrier()


def stage_conf(k, l, hT, wb):
    nc, P, ps = k.nc, k.P, k.ps
    wv = k.w_in[l].rearrange("(kc p) n -> p kc n", p=128)
    with ExitStack() as st:
        sb = lambda name, shape, dt: st.enter_context(nc.sbuf_tensor(_uniq(name), list(shape), dt))
        cdw = sb("cdw", [128, 4, 31], F32)
        cvec = sb("cvec", [128, 4, 3], F32)
        u = sb("cu", [128, T + 30], F32)
        sg = sb("csg", [128, T], F32)
        accD = sb("caccD", [128, T], F32)
        accP = sb("caccP", [128, T], F32)
        v = sb("cv", [128, 4, T], F32)
        vb = sb("cvb", [128, 4, 512], BF16)
        vsq = sb("cvsq", [128, 4, 512], BF16)
        mean = sb("cmean", [128, 512], F32)
        msq = sb("cmsq", [128, 512], F32)
        rstd = sb("crstd", [128, 512], F32)
        tt_ = [sb(f"ctt{i}", [128, 512], F32) for i in range(2)]
        yb = [sb(f"cyb{i}", [128, 512], BF16) for i in range(2)]
        P.dma("sp", lambda e: e.dma_start(out=cdw[:], in_=k.cdw[l].rearrange("j p t -> p j t")), writes=["cdw"])
        P.dma("sp", lambda e: e.dma_start(out=cvec[:], in_=k.cvec[l].rearrange("j p t -> p j t")), writes=["cvec"])
        P.dve(lambda e: e.memset(u[:, 0:30], 0.0), writes=["cu"])
        P.dma("pool", lambda e: e.dma_start(out=wb[0][:], in_=wv[:, :, 2048:2560]), writes=[("wb", 0)])
        P.dma("pool", lambda e: e.dma_start(out=wb[1][:], in_=wv[:, :, 2560:3072]), writes=[("wb", 1)])
        n = 0
        for j in range(4):
            for half in range(2):
                s_ = n % 2
                n += 1
                for wi, b0 in ((0, 4 * s_), (1, 4 * s_ + 2)):
                    for kc in range(16):
                        for q in range(2):
                            P.pe(lambda e, kc=kc, q=q, b0=b0, wi=wi, half=half, j=j: e.matmul(
                                ps[:, b0 + q, :], wb[wi][:, kc, j * 128:(j + 1) * 128],
                                hT[:, kc, half * 1024 + q * 512: half * 1024 + (q + 1) * 512],
                                start=(kc == 0), stop=(kc == 15)),
                                reads=[("wb", wi), "hT"], writes=[("ps", b0 + q)])
                pa = ps[:, 4 * s_:4 * s_ + 2, :].rearrange("p a b -> p (a b)")
                pg = ps[:, 4 * s_ + 2:4 * s_ + 4, :].rearrange("p a b -> p (a b)")
                cols = slice(half * 1024, (half + 1) * 1024)
                P.act(lambda e, pg=pg, cols=cols: e.activation(out=sg[:, cols], in_=pg, func=AF.Sigmoid),
                      reads=ps_keys(4 * s_ + 2, 2), writes=["csg"])
                P.dve(lambda e, pa=pa, cols=cols, half=half: e.tensor_tensor(
                    out=u[:, 30 + half * 1024: 30 + (half + 1) * 1024], in0=pa, in1=sg[:, cols], op=ALU.mult),
                    reads=ps_keys(4 * s_, 2) + ["csg"], writes=["cu"])
            P.dve(lambda e, j=j: e.tensor_scalar(out=accD[:], in0=u[:, 0:T], scalar1=cdw[:, j, 0:1], scalar2=cvec[:, j, 0:1],
                                                 op0=ALU.mult, op1=ALU.add),
                  reads=["cu", "cdw", "cvec"], writes=["caccD"])
            for t in range(1, 16):
                P.dve(lambda e, j=j, t=t: e.scalar_tensor_tensor(out=accD[:], in0=u[:, t:t + T], scalar=cdw[:, j, t:t + 1],
                                                                 in1=accD[:], op0=ALU.mult, op1=ALU.add),
                      reads=["cu", "cdw", "caccD"], writes=["caccD"])
            P.dve(lambda e, j=j: e.tensor_scalar(out=accP[:], in0=u[:, 16:16 + T], scalar1=cdw[:, j, 16:17], scalar2=None,
                                                  op0=ALU.mult),
                   reads=["cu", "cdw"], writes=["caccP"])
            for t in range(17, 31):
                P.dve(lambda e, j=j, t=t: e.scalar_tensor_tensor(out=accP[:], in0=u[:, t:t + T], scalar=cdw[:, j, t:t + 1],
                                                                  in1=accP[:], op0=ALU.mult, op1=ALU.add),
                       reads=["cu", "cdw", "caccP"], writes=["caccP"])
            P.dve(lambda e, j=j: e.tensor_tensor(out=v[:, j, :], in0=accD[:], in1=accP[:], op=ALU.add),
                  reads=["caccD", "caccP"], writes=[("cv", j)])
        for q in range(4):
            qs = slice(q * 512, (q + 1) * 512)
            for j in range(4):
                P.act(lambda e, j=j, qs=qs: e.activation(out=vb[:, j, :], in_=v[:, j, qs], func=AF.Copy),
                      reads=[("cv", j)], writes=["cvb"])
                P.act(lambda e, j=j, qs=qs: e.activation(out=vsq[:, j, :], in_=v[:, j, qs], func=AF.Square),
                      reads=[("cv", j)], writes=["cvsq"])
            for j in range(4):
                P.pe(lambda e, j=j: e.matmul(ps[:, 0, :], k.ones_bf[:], vb[:, j, :], start=(j == 0), stop=(j == 3)),
                     reads=["cvb", "ones_bf"], writes=ps_keys(0))
            for j in range(4):
                P.pe(lambda e, j=j: e.matmul(ps[:, 1, :], k.ones_bf[:], vsq[:, j, :], start=(j == 0), stop=(j == 3)),
                     reads=["cvsq", "ones_bf"], writes=ps_keys(1))
            P.dve(lambda e: e.tensor_scalar(out=mean[:], in0=ps[:, 0, :], scalar1=1.0 / 512, scalar2=None, op0=ALU.mult),
                  reads=ps_keys(0), writes=["cmean"])
            P.dve(lambda e: e.tensor_tensor(out=msq[:], in0=mean[:], in1=mean[:], op=ALU.mult),
                  reads=["cmean"], writes=["cmsq"])
            P.dve(lambda e: e.scalar_tensor_tensor(out=msq[:], in0=ps[:, 1, :], scalar=1.0 / 512, in1=msq[:],
                                                   op0=ALU.mult, op1=ALU.subtract),
                  reads=ps_keys(1) + ["cmsq"], writes=["cmsq"])
            P.act(lambda e: e.activation(out=rstd[:], in_=msq[:], func=AF.Sqrt, bias=EPS, scale=1.0),
                  reads=["cmsq"], writes=["crstd"])
            P.dve(lambda e: e.reciprocal(out=rstd[:], in_=rstd[:]), reads=["crstd"], writes=["crstd"])
            for j in range(4):
                i = j % 2
                P.dve(lambda e, j=j, i=i, qs=qs: e.tensor_tensor(out=tt_[i][:], in0=v[:, j, qs], in1=mean[:], op=ALU.subtract),
                      reads=[("cv", j), "cmean"], writes=[("ctt", i)])
                P.dve(lambda e, i=i: e.tensor_tensor(out=tt_[i][:], in0=tt_[i][:], in1=rstd[:], op=ALU.mult),
                      reads=[("ctt", i), "crstd"], writes=[("ctt", i)])
                P.act(lambda e, j=j, i=i: e.activation(out=yb[i][:], in_=tt_[i][:], func=AF.Silu,
                                                       scale=cvec[:, j, 1:2], bias=cvec[:, j, 2:3]),
                      reads=[("ctt", i), "cvec"], writes=[("cyb", i)])
                P.dma("sp", lambda e, j=j, i=i, qs=qs: e.dma_start(out=k.yT[512 + j * 128: 512 + (j + 1) * 128, qs], in_=yb[i][:]),
                      reads=[("cyb", i)], writes=[("yT", 4 + j, q)])


def make_in_maps(inp, batches, LW=L):
    f = lambda a: np.ascontiguousarray(np.asarray(a, dtype=np.float32))
    shared = {}
    shared["w_in"] = f(np.asarray(inp["w_in"])[:LW][:, :, PERM])
    shared["w_out"] = f(np.asarray(inp["w_out"])[:LW])
    shared["w_up"] = f(np.asarray(inp["w_up"])[:LW])
    shared["w_down"] = f(np.asarray(inp["w_down"])[:LW])
    shared["nw1"] = f(np.asarray(inp["attn_norm_w"]).reshape(L, 16, 128).transpose(0, 2, 1))
    shared["nw2"] = f(np.asarray(inp["ffn_norm_w"]).reshape(L, 16, 128).transpose(0, 2, 1))
    shared["nwf"] = f(np.asarray(inp["final_norm_w"]).reshape(16, 128).T)
    shared["gbias"] = f(np.concatenate([np.asarray(inp["mlstm_i_bias"]), np.asarray(inp["mlstm_f_bias"])], axis=1))
    shared["mnw"] = f(inp["mlstm_norm_w"])
    shared["cdw"] = f(np.asarray(inp["conv_dw_w"]).transpose(0, 2, 1).reshape(L, 4, 128, 31))
    shared["cvec"] = f(np.stack([np.asarray(inp["conv_dw_b"]), np.asarray(inp["conv_ln_w"]),
                                 np.asarray(inp["conv_ln_b"])], axis=-1).reshape(L, 4, 128, 3))
    shared["peT"] = f(np.stack([np.asarray(inp["nsa_cmp_pe_k"]), np.asarray(inp["nsa_cmp_pe_v"])], axis=1).transpose(0, 1, 3, 2))
    shared["cmpw"] = f(np.stack([np.asarray(inp["nsa_cmp_w_k"]), np.asarray(inp["nsa_cmp_w_v"])], axis=1))
    shared["fdw"] = f(np.asarray(inp["ffn_dw_w"]).reshape(L, 3, 88, 128).transpose(0, 3, 2, 1))
    for n, v in host_consts().items():
        shared["c_" + n] = np.ascontiguousarray(v)
    maps = []
    for b in batches:
        m = dict(shared)
        m["xT"] = f(np.asarray(inp["x"])[b].T)
        maps.append(m)
    return maps


def stage_mlstm(k, l):
    nc, P, ps = k.nc, k.P, k.ps
    SK = 128 ** -0.5
    with ExitStack() as st:
        sb = lambda name, shape, dt: st.enter_context(nc.sbuf_tensor(_uniq(name), list(shape), dt))
        qT = sb("m_qT", [128, 4, T], BF16)
        kT = sb("m_kT", [128, 4, T], BF16)
        ktm = sb("m_ktm", [128, 16, 512], BF16)
        vtm = sb("m_vtm", [128, 16, 512], BF16)
        otm = sb("m_otm", [128, 16, 512], BF16)
        vext = sb("m_vext", [128, 16, 4, 132], BF16)
        g = sb("m_g", [128, 16, 32], F32)
        gb = sb("m_gb", [128, 8], F32)
        mnw = sb("m_mnw", [128, 512], F32)
        tri01 = sb("m_tri01", [128, 512], F32)
        triU = sb("m_triU", [128, 128], F32)
        ii = sb("m_ii", [128, 16, 4], F32)
        ff = sb("m_ff", [128, 16, 4], F32)
        lf = sb("m_lf", [128, 64], F32)
        aa = sb("m_a", [128, 64], F32)
        ea = sb("m_ea", [128, 64], F32)
        ebs = sb("m_ebs", [128, 64], F32)
        ebl = sb("m_ebl", [128, 64], F32)
        Cf = sb("m_Cf", [128, 4, 132], F32)
        tmpC = sb("m_tmpC", [128, 4, 132], F32)
        Cb = sb("m_Cb", [128, 4, 132], BF16)
        Pm = [sb(f"m_Pm{i}", [128, 512], BF16) for i in range(2)]
        sm = [sb(f"m_sm{i}", [128, 8, 4], F32) for i in range(2)]
        hh = [sb(f"m_hh{i}", [128, 4, 128], F32) for i in range(2)]
        sq = sb("m_sq", [128, 4, 128], F32)
        sgo = sb("m_sgo", [128, 512], F32)
        ya = sb("m_ya", [128, 512], BF16)
        yaT = [sb(f"m_yaT{i}", [128, 512], BF16) for i in range(2)]
        ld = lambda q, out, in_, key, rd=(): P.dma(q, lambda e: e.dma_start(out=out, in_=in_), reads=list(rd), writes=[key])
        ld("sp", qT[:], k.mqT.rearrange("(h d) t -> d h t", d=128), "m_qT")
        ld("sp", kT[:], k.mkT.rearrange("(h d) t -> d h t", d=128), "m_kT")
        ld("sp", ktm[:], k.mk_tm.rearrange("(c p) n -> p c n", p=128), "m_ktm")
        ld("sp", vtm[:], k.mv_tm.rearrange("(c p) n -> p c n", p=128), "m_vtm")
        ld("sp", otm[:], k.mo_tm.rearrange("(c p) n -> p c n", p=128), "m_otm")
        ld("sp", g[:], k.gates.rearrange("(c p) n -> p c n", p=128), "m_g")
        ld("sp", gb[:], k.gbias[l].partition_broadcast(128), "m_gb")
        ld("sp", mnw[:], k.mnw[l].partition_broadcast(128), "m_mnw")
        ld("sp", tri01[:], k.consts["tri01"], "m_tri01")
        ld("sp", triU[:], k.consts["triU"], "m_triU")
        bc = lambda ap, shape: ap.unsqueeze(len(ap.shape)).to_broadcast(shape)
        P.dve(lambda e: e.tensor_tensor(out=ii[:], in0=g[:, :, 0:4], in1=gb[:, 0:4].unsqueeze(1).to_broadcast([128, 16, 4]), op=ALU.add),
              reads=["m_g", "m_gb"], writes=["m_ii"])
        P.dve(lambda e: e.tensor_tensor(out=ff[:], in0=g[:, :, 4:8], in1=gb[:, 4:8].unsqueeze(1).to_broadcast([128, 16, 4]), op=ALU.add),
              reads=["m_g", "m_gb"], writes=["m_ff"])
        ffv = ff[:].rearrange("p c h -> p (c h)")
        iiv = ii[:].rearrange("p c h -> p (c h)")
        P.act(lambda e: e.activation(out=lf[:], in_=ffv, func=AF.Exp, scale=-1.0), reads=["m_ff"], writes=["m_lf"])
        P.act(lambda e: e.activation(out=lf[:], in_=lf[:], func=AF.Ln, bias=1.0, scale=1.0), reads=["m_lf"], writes=["m_lf"])
        P.dve(lambda e: e.tensor_scalar(out=lf[:], in0=lf[:], scalar1=-1.0, scalar2=None, op0=ALU.mult), reads=["m_lf"], writes=["m_lf"])
        P.pe(lambda e: e.matmul(ps[:, 0, 0:64], triU[:], lf[:], start=True, stop=True), reads=["m_triU", "m_lf"], writes=ps_keys(0))
        P.pe(lambda e: e.matmul(ps[:, 1, 0:64], k.ones_f[:], lf[:], start=True, stop=True), reads=["ones_f", "m_lf"], writes=ps_keys(1))
        P.dve(lambda e: e.tensor_tensor(out=aa[:], in0=iiv, in1=ps[:, 0, 0:64], op=ALU.subtract), reads=["m_ii"] + ps_keys(0), writes=["m_a"])
        P.act(lambda e: e.activation(out=ea[:], in_=aa[:], func=AF.Exp), reads=["m_a"], writes=["m_ea"])
        P.act(lambda e: e.activation(out=ebs[:], in_=ps[:, 0, 0:64], func=AF.Exp), reads=ps_keys(0), writes=["m_ebs"])
        P.dve(lambda e: e.tensor_scalar(out=ebs[:], in0=ebs[:], scalar1=SK, scalar2=None, op0=ALU.mult), reads=["m_ebs"], writes=["m_ebs"])
        P.act(lambda e: e.activation(out=ebl[:], in_=ps[:, 1, 0:64], func=AF.Exp), reads=ps_keys(1), writes=["m_ebl"])
        vx = vext[:].rearrange("p c h x -> p (c h) x")
        P.dve(lambda e: e.tensor_tensor(out=vx[:, :, 0:128], in0=vtm[:].rearrange("p c (h x) -> p (c h) x", x=128),
                                        in1=bc(ea[:], [128, 64, 128]), op=ALU.mult),
              reads=["m_vtm", "m_ea"], writes=["m_vext"])
        P.dve(lambda e: e.tensor_copy(out=vx[:, :, 128:129], in_=ea[:].unsqueeze(2)), reads=["m_ea"], writes=["m_vext"])
        P.dve(lambda e: e.memset(Cf[:], 0.0), writes=["m_Cf"])
        for c in range(16):
            cs = slice(c * 128, (c + 1) * 128)
            par = c % 2
            bO = 2 + 2 * par
            for h in range(4):
                P.pe(lambda e, h=h, cs=cs: e.matmul(ps[:, 0, h * 128:(h + 1) * 128], kT[:, h, cs], qT[:, h, cs], start=True, stop=True),
                     reads=["m_kT", "m_qT"], writes=ps_keys(0))
            P.dve(lambda e, par=par: e.tensor_tensor(out=Pm[par][:], in0=ps[:, 0, :], in1=tri01[:], op=ALU.mult),
                  reads=ps_keys(0) + ["m_tri01"], writes=[("m_Pm", par)])
            for h in range(4):
                ov = ps[:, bO + h // 2, (h % 2) * 256:(h % 2) * 256 + 129]
                P.pe(lambda e, h=h, ov=ov, par=par, c=c: e.matmul(ov, Pm[par][:, h * 128:(h + 1) * 128], vext[:, c, h, 0:129],
                                                                  start=True, stop=(c == 0)),
                     reads=[("m_Pm", par), "m_vext"], writes=ps_keys(bO + h // 2))
                if c > 0:
                    P.pe(lambda e, h=h, ov=ov, cs=cs: e.matmul(ov, qT[:, h, cs], Cb[:, h, 0:129], start=False, stop=True),
                         reads=["m_qT", "m_Cb"], writes=ps_keys(bO + h // 2))
            if c < 15:
                for h in range(4):
                    kvv = ps[:, 6 + h // 2, (h % 2) * 256:(h % 2) * 256 + 129]
                    P.pe(lambda e, h=h, kvv=kvv, c=c: e.matmul(kvv, ktm[:, c, h * 128:(h + 1) * 128], vext[:, c, h, 0:129],
                                                               start=True, stop=True),
                         reads=["m_ktm", "m_vext"], writes=ps_keys(6 + h // 2))
                kv4 = ps[:, 6:8, :].rearrange("p a (b x) -> p (a b) x", x=256)[:, :, 0:129]
                P.dve(lambda e, kv4=kv4: e.tensor_tensor(out=tmpC[:, :, 0:129], in0=Cf[:, :, 0:129], in1=kv4, op=ALU.add),
                      reads=["m_Cf"] + ps_keys(6, 2), writes=["m_tmpC"])
                P.dve(lambda e, c=c: e.tensor_tensor(out=Cf[:, :, 0:129], in0=tmpC[:, :, 0:129],
                                                     in1=bc(ebl[:, c * 4:(c + 1) * 4], [128, 4, 129]), op=ALU.mult),
                      reads=["m_tmpC", "m_ebl"], writes=["m_Cf"])
                P.act(lambda e: e.activation(out=Cb[:, :, 0:129], in_=Cf[:, :, 0:129], func=AF.Copy), reads=["m_Cf"], writes=["m_Cb"])
            o4 = ps[:, bO:bO + 2, :].rearrange("p a (b x) -> p (a b) x", x=256)
            s_ = sm[par]
            ebc = ebs[:, c * 4:(c + 1) * 4]
            rk = ps_keys(bO, 2)
            P.dve(lambda e, o4=o4, s_=s_, ebc=ebc: e.tensor_tensor(out=s_[:, 0, :], in0=o4[:, :, 128], in1=ebc, op=ALU.mult),
                  reads=rk + ["m_ebs"], writes=[("m_sm", par)])
            P.dve(lambda e, s_=s_: e.tensor_scalar(out=s_[:, 1, :], in0=s_[:, 0, :], scalar1=-1.0, scalar2=None, op0=ALU.mult),
                  reads=[("m_sm", par)], writes=[("m_sm", par)])
            P.dve(lambda e, s_=s_: e.tensor_tensor(out=s_[:, 1, :], in0=s_[:, 1, :], in1=s_[:, 0, :], op=ALU.max),
                  reads=[("m_sm", par)], writes=[("m_sm", par)])
            P.dve(lambda e, s_=s_: e.tensor_scalar(out=s_[:, 1, :], in0=s_[:, 1, :], scalar1=1.0, scalar2=None, op0=ALU.max),
                  reads=[("m_sm", par)], writes=[("m_sm", par)])
            P.dve(lambda e, s_=s_: e.reciprocal(out=s_[:, 1, :], in_=s_[:, 1, :]), reads=[("m_sm", par)], writes=[("m_sm", par)])
            P.dve(lambda e, s_=s_, ebc=ebc: e.tensor_tensor(out=s_[:, 2, :], in0=s_[:, 1, :], in1=ebc, op=ALU.mult),
                  reads=[("m_sm", par), "m_ebs"], writes=[("m_sm", par)])
            H = hh[par]
            P.dve(lambda e, o4=o4, s_=s_, H=H: e.tensor_tensor(out=H[:], in0=o4[:, :, 0:128], in1=bc(s_[:, 2, :], [128, 4, 128]), op=ALU.mult),
                  reads=rk + [("m_sm", par)], writes=[("m_hh", par)])
            P.dve(lambda e, s_=s_, H=H: e.tensor_reduce(out=s_[:, 3, :], in_=H[:], axis=AX.X, op=ALU.add),
                  reads=[("m_hh", par)], writes=[("m_sm", par)])
            P.act(lambda e, H=H: e.activation(out=sq[:], in_=H[:], func=AF.Square), reads=[("m_hh", par)], writes=["m_sq"])
            P.dve(lambda e, s_=s_: e.tensor_reduce(out=s_[:, 4, :], in_=sq[:], axis=AX.X, op=ALU.add),
                  reads=["m_sq"], writes=[("m_sm", par)])
            P.dve(lambda e, s_=s_: e.tensor_scalar(out=s_[:, 3, :], in0=s_[:, 3, :], scalar1=1.0 / 128, scalar2=None, op0=ALU.mult),
                  reads=[("m_sm", par)], writes=[("m_sm", par)])
            P.dve(lambda e, s_=s_: e.tensor_tensor(out=s_[:, 5, :], in0=s_[:, 3, :], in1=s_[:, 3, :], op=ALU.mult),
                  reads=[("m_sm", par)], writes=[("m_sm", par)])
            P.dve(lambda e, s_=s_: e.scalar_tensor_tensor(out=s_[:, 5, :], in0=s_[:, 4, :], scalar=1.0 / 128, in1=s_[:, 5, :],
                                                          op0=ALU.mult, op1=ALU.subtract),
                  reads=[("m_sm", par)], writes=[("m_sm", par)])
            P.act(lambda e, s_=s_: e.activation(out=s_[:, 6, :], in_=s_[:, 5, :], func=AF.Sqrt, bias=EPS, scale=1.0),
                  reads=[("m_sm", par)], writes=[("m_sm", par)])
            P.dve(lambda e, s_=s_: e.reciprocal(out=s_[:, 6, :], in_=s_[:, 6, :]), reads=[("m_sm", par)], writes=[("m_sm", par)])
            P.dve(lambda e, s_=s_, H=H: e.tensor_tensor(out=H[:], in0=H[:], in1=bc(s_[:, 3, :], [128, 4, 128]), op=ALU.subtract),
                  reads=[("m_hh", par), ("m_sm", par)], writes=[("m_hh", par)])
            P.dve(lambda e, s_=s_, H=H: e.tensor_tensor(out=H[:], in0=H[:], in1=bc(s_[:, 6, :], [128, 4, 128]), op=ALU.mult),
                  reads=[("m_hh", par), ("m_sm", par)], writes=[("m_hh", par)])
            Hv = H[:].rearrange("p h x -> p (h x)")
            P.dve(lambda e, Hv=Hv: e.tensor_tensor(out=Hv, in0=Hv, in1=mnw[:], op=ALU.mult),
                  reads=[("m_hh", par), "m_mnw"], writes=[("m_hh", par)])
            P.act(lambda e, c=c: e.activation(out=sgo[:], in_=otm[:, c, :], func=AF.Sigmoid), reads=["m_otm"], writes=["m_sgo"])
            P.dve(lambda e, Hv=Hv: e.tensor_tensor(out=ya[:], in0=Hv, in1=sgo[:], op=ALU.mult),
                  reads=[("m_hh", par), "m_sgo"], writes=["m_ya"])
            pT = ps[:, 1, :].bitcast(BF16)
            for h in range(4):
                P.pe(lambda e, h=h, pT=pT: e.transpose(pT[:, h * 128:(h + 1) * 128], ya[:, h * 128:(h + 1) * 128], k.ident[:]),
                     reads=["m_ya", "ident"], writes=ps_keys(1))
            P.act(lambda e, pT=pT, par=par: e.activation(out=yaT[par][:], in_=pT[:, 0:512], func=AF.Copy),
                  reads=ps_keys(1), writes=[("m_yaT", par)])
            P.dma("sp", lambda e, par=par, cs=cs: e.dma_start(out=k.yT[0:512, cs].rearrange("(h d) t -> d h t", d=128),
                                                              in_=yaT[par][:].rearrange("p (h t) -> p h t", t=128)),
                  reads=[("m_yaT", par)], writes=[("yT", "a", c)])
    P.barrier()


def stage_wout(k, l, x_src, xkey):
    nc, P, ps = k.nc, k.P, k.ps
    wv = k.w_out[l].rearrange("(kc p) n -> p kc n", p=128)
    with ExitStack() as st:
        sb = lambda name, shape, dt: st.enter_context(nc.sbuf_tensor(_uniq(name), list(shape), dt))
        yTs = sb("o_yT", [128, 16, T], BF16)
        wb = [sb(f"o_wb{i}", [128, 16, 512], BF16) for i in range(2)]
        xold = [sb(f"o_xold{i}", [128, 1024], F32) for i in range(3)]
        P.dma("sp", lambda e: e.dma_start(out=yTs[:], in_=k.yT.rearrange("(kc p) t -> p kc t", p=128)),
              reads=[("yT", "all")], writes=["o_yT"])
        n = 0
        for gi in range(4):
            P.dma("pool", lambda e, gi=gi: e.dma_start(out=wb[gi % 2][:], in_=wv[:, :, gi * 512:(gi + 1) * 512]),
                  writes=[("o_wb", gi % 2)])
            for c in range(4):
                dc = gi * 4 + c
                for half in range(2):
                    b0 = (n % 4) * 2
                    xi = n % 3
                    n += 1
                    cols = slice(half * 1024, (half + 1) * 1024)
                    P.dma("sp", lambda e, xi=xi, dc=dc, cols=cols: e.dma_start(out=xold[xi][:], in_=x_src[dc * 128:(dc + 1) * 128, cols]),
                          reads=[(xkey, dc)], writes=[("o_xold", xi)])
                    for kc in range(16):
                        for q in range(2):
                            P.pe(lambda e, kc=kc, q=q, b0=b0, gi=gi, c=c, half=half: e.matmul(
                                ps[:, b0 + q, :], wb[gi % 2][:, kc, c * 128:(c + 1) * 128],
                                yTs[:, kc, half * 1024 + q * 512: half * 1024 + (q + 1) * 512],
                                start=(kc == 0), stop=(kc == 15)),
                                reads=[("o_wb", gi % 2), "o_yT"], writes=[("ps", b0 + q)])
                    psv = ps[:, b0:b0 + 2, :].rearrange("p a b -> p (a b)")
                    P.dve(lambda e, psv=psv, xi=xi: e.tensor_tensor(out=xold[xi][:], in0=psv, in1=xold[xi][:], op=ALU.add),
                          reads=ps_keys(b0, 2) + [("o_xold", xi)], writes=[("o_xold", xi)])
                    P.dma("sp", lambda e, xi=xi, dc=dc, cols=cols: e.dma_start(out=k.xm[dc * 128:(dc + 1) * 128, cols], in_=xold[xi][:]),
                          reads=[("o_xold", xi)], writes=[("xm", dc)])
    P.barrier()


def stage_ffn(k, l):
    nc, P, ps = k.nc, k.P, k.ps
    wu = k.w_up[l].rearrange("(kc p) n -> p kc n", p=128)
    wd = k.w_down[l].rearrange("(fc p) n -> p fc n", p=128)
    with ExitStack() as st:
        sb = lambda name, shape, dt: st.enter_context(nc.sbuf_tensor(_uniq(name), list(shape), dt))
        hT = sb("f_hT", [128, 16, T], BF16)
        with ExitStack() as st2:
            rms_to_hT(k, st2, l, k.xm, "xm", k.nw2[l], hT)
        P.barrier()
        with ExitStack() as st3:
            sb3 = lambda name, shape, dt: st3.enter_context(nc.sbuf_tensor(_uniq(name), list(shape), dt))
            wg = [sb3(f"f_wg{i}", [128, 16, 512], BF16) for i in range(2)]
            wvv = [sb3(f"f_wv{i}", [128, 16, 512], BF16) for i in range(2)]
            fdw = sb3("f_fdw", [128, 88, 3], F32)
            ug = [sb3(f"f_ug{i}", [128, T + 2], F32) for i in range(2)]
            uv = [sb3(f"f_uv{i}", [128, T + 2], F32) for i in range(2)]
            cg = sb3("f_cg", [128, T], F32)
            cv = sb3("f_cv", [128, T], F32)
            ab = [sb3(f"f_ab{i}", [128, T], BF16) for i in range(2)]
            P.dma("sp", lambda e: e.dma_start(out=fdw[:], in_=k.fdw[l]), writes=["f_fdw"])
            for i in range(2):
                P.dve(lambda e, i=i: e.memset(ug[i][:, 0:2], 0.0), writes=[("f_ug", i)])
                P.dve(lambda e, i=i: e.memset(uv[i][:, 0:2], 0.0), writes=[("f_uv", i)])
            n = 0
            for gi in range(11):
                P.dma("pool", lambda e, gi=gi: e.dma_start(out=wg[gi % 2][:], in_=wu[:, :, gi * 512:(gi + 1) * 512]),
                      writes=[("f_wg", gi % 2)])
                P.dma("pool", lambda e, gi=gi: e.dma_start(out=wvv[gi % 2][:], in_=wu[:, :, DFF + gi * 512:DFF + (gi + 1) * 512]),
                      writes=[("f_wv", gi % 2)])
                for c4 in range(4):
                    c = gi * 4 + c4
                    ui = c % 2
                    for half in range(2):
                        s_ = n % 2
                        n += 1
                        for (wt, wkey, b0, dst, dkey) in ((wg, "f_wg", 4 * s_, ug, "f_ug"), (wvv, "f_wv", 4 * s_ + 2, uv, "f_uv")):
                            for kc in range(16):
                                for q in range(2):
                                    P.pe(lambda e, kc=kc, q=q, b0=b0, wt=wt, gi=gi, c4=c4, half=half: e.matmul(
                                        ps[:, b0 + q, :], wt[gi % 2][:, kc, c4 * 128:(c4 + 1) * 128],
                                        hT[:, kc, half * 1024 + q * 512: half * 1024 + (q + 1) * 512],
                                        start=(kc == 0), stop=(kc == 15)),
                                        reads=[(wkey, gi % 2), "hT"], writes=[("ps", b0 + q)])
                            psv = ps[:, b0:b0 + 2, :].rearrange("p a b -> p (a b)")
                            P.act(lambda e, psv=psv, dst=dst, ui=ui, half=half: e.activation(
                                out=dst[ui][:, 2 + half * 1024: 2 + (half + 1) * 1024], in_=psv, func=AF.Copy),
                                reads=ps_keys(b0, 2), writes=[(dkey, ui)])
                    P.dve(lambda e, c=c, ui=ui: e.tensor_scalar(out=cg[:], in0=ug[ui][:, 2:2 + T], scalar1=fdw[:, c, 2:3], scalar2=None, op0=ALU.mult),
                          reads=[("f_ug", ui), "f_fdw"], writes=["f_cg"])
                    for t in (1, 0):
                        P.dve(lambda e, c=c, ui=ui, t=t: e.scalar_tensor_tensor(out=cg[:], in0=ug[ui][:, t:t + T], scalar=fdw[:, c, t:t + 1],
                                                                                in1=cg[:], op0=ALU.mult, op1=ALU.add),
                              reads=[("f_ug", ui), "f_fdw", "f_cg"], writes=["f_cg"])
                    P.dve(lambda e, c=c, ui=ui: e.tensor_scalar(out=cv[:], in0=uv[ui][:, 2:2 + T], scalar1=fdw[:, 44 + c, 2:3], scalar2=None, op0=ALU.mult),
                           reads=[("f_uv", ui), "f_fdw"], writes=["f_cv"])
                    for t in (1, 0):
                        P.dve(lambda e, c=c, ui=ui, t=t: e.scalar_tensor_tensor(out=cv[:], in0=uv[ui][:, t:t + T], scalar=fdw[:, 44 + c, t:t + 1],
                                                                                 in1=cv[:], op0=ALU.mult, op1=ALU.add),
                               reads=[("f_uv", ui), "f_fdw", "f_cv"], writes=["f_cv"])
                    P.act(lambda e: e.activation(out=cg[:], in_=cg[:], func=AF.Silu), reads=["f_cg"], writes=["f_cg"])
                    P.dve(lambda e, ui=ui: e.tensor_tensor(out=ab[ui][:], in0=cg[:], in1=cv[:], op=ALU.mult),
                          reads=["f_cg", "f_cv"], writes=[("f_ab", ui)])
                    P.dma("sp", lambda e, ui=ui, c=c: e.dma_start(out=k.aT[c * 128:(c + 1) * 128, :], in_=ab[ui][:]),
                          reads=[("f_ab", ui)], writes=[("aT", c)])
        P.barrier()
    with ExitStack() as st:
        sb = lambda name, shape, dt: st.enter_context(nc.sbuf_tensor(_uniq(name), list(shape), dt))
        aTs = sb("d_aT", [128, 44, 1024], BF16)
        wdb = [sb(f"d_wd{i}", [128, 44, 128], BF16) for i in range(3)]
        xold = [sb(f"d_xold{i}", [128, 1024], F32) for i in range(3)]
        n = 0
        for half in range(2):
            cols = slice(half * 1024, (half + 1) * 1024)
            P.dma("sp", lambda e, cols=cols: e.dma_start(out=aTs[:], in_=k.aT.rearrange("(fc p) t -> p fc t", p=128)[:, :, cols]),
                  reads=[("aT", "all")], writes=["d_aT"])
            for dc in range(16):
                wi = n % 3
                b0 = (n % 4) * 2
                n += 1
                P.dma("pool", lambda e, wi=wi, dc=dc: e.dma_start(out=wdb[wi][:], in_=wd[:, :, dc * 128:(dc + 1) * 128]),
                      writes=[("d_wd", wi)])
                P.dma("sp", lambda e, wi=wi, dc=dc, cols=cols: e.dma_start(out=xold[wi][:], in_=k.xm[dc * 128:(dc + 1) * 128, cols]),
                      reads=[("xm", dc)], writes=[("d_xold", wi)])
                for fc in range(44):
                    for q in range(2):
                        P.pe(lambda e, fc=fc, q=q, b0=b0, wi=wi: e.matmul(
                            ps[:, b0 + q, :], wdb[wi][:, fc, :], aTs[:, fc, q * 512:(q + 1) * 512],
                            start=(fc == 0), stop=(fc == 43)),
                            reads=[("d_wd", wi), "d_aT"], writes=[("ps", b0 + q)])
                psv = ps[:, b0:b0 + 2, :].rearrange("p a b -> p (a b)")
                P.dve(lambda e, psv=psv, wi=wi: e.tensor_tensor(out=xold[wi][:], in0=psv, in1=xold[wi][:], op=ALU.add),
                      reads=ps_keys(b0, 2) + [("d_xold", wi)], writes=[("d_xold", wi)])
                P.dma("sp", lambda e, wi=wi, dc=dc, cols=cols: e.dma_start(out=k.xo[dc * 128:(dc + 1) * 128, cols], in_=xold[wi][:]),
                      reads=[("d_xold", wi)], writes=[("xo", dc)])
    P.barrier()


def stage_final(k):
    nc, P, ps = k.nc, k.P, k.ps
    with ExitStack() as st:
        sb = lambda name, shape, dt: st.enter_context(nc.sbuf_tensor(_uniq(name), list(shape), dt))
        xin = [sb(f"fn_xin{i}", [128, T], F32) for i in range(2)]
        sq = [sb(f"fn_sq{i}", [128, T], BF16) for i in range(2)]
        rstd = sb("fn_rstd", [128, T], F32)
        nw = sb("fn_nw", [128, 16], F32)
        P.dma("sp", lambda e: e.dma_start(out=nw[:], in_=k.nwf), writes=["fn_nw"])
        for kc in range(16):
            i = kc % 2
            P.dma("sp", lambda e, kc=kc, i=i: e.dma_start(out=xin[i][:], in_=k.xo[kc * 128:(kc + 1) * 128, :]),
                  reads=[("xo", kc)], writes=[("fn_xin", i)])
            P.act(lambda e, i=i: e.activation(out=sq[i][:], in_=xin[i][:], func=AF.Square),
                  reads=[("fn_xin", i)], writes=[("fn_sq", i)])
            for q in range(4):
                P.pe(lambda e, kc=kc, i=i, q=q: e.matmul(ps[:, q, :], k.ones_bf[:], sq[i][:, q * 512:(q + 1) * 512],
                                                          start=(kc == 0), stop=(kc == 15)),
                     reads=[("fn_sq", i), "ones_bf"], writes=ps_keys(q))
        for q in range(4):
            P.act(lambda e, q=q: e.activation(out=rstd[:, q * 512:(q + 1) * 512], in_=ps[:, q, :], func=AF.Sqrt,
                                              bias=EPS, scale=1.0 / D),
                  reads=ps_keys(q), writes=["fn_rstd"])
        P.dve(lambda e: e.reciprocal(out=rstd[:], in_=rstd[:]), reads=["fn_rstd"], writes=["fn_rstd"])
        for kc in range(16):
            i = kc % 2
            P.dma("sp", lambda e, kc=kc, i=i: e.dma_start(out=xin[i][:], in_=k.xo[kc * 128:(kc + 1) * 128, :]),
                  reads=[("xo", kc)], writes=[("fn_xin", i)])
            P.dve(lambda e, kc=kc, i=i: e.scalar_tensor_tensor(out=xin[i][:], in0=xin[i][:], scalar=nw[:, kc:kc + 1],
                                                               in1=rstd[:], op0=ALU.mult, op1=ALU.mult),
                  reads=[("fn_xin", i), "fn_rstd", "fn_nw"], writes=[("fn_xin", i)])
            P.dma("sp", lambda e, kc=kc, i=i: e.dma_start(out=k.outT[kc * 128:(kc + 1) * 128, :], in_=xin[i][:]),
                  reads=[("fn_xin", i)], writes=[("outT", kc)])


def kernel(**inputs):
    nc, k = build(n_layers=L, stages=("s1", "conf", "mlstm", "nsa", "wout", "ffn", "final"))
    maps = make_in_maps(inputs, [0, 1, 2, 3])
    res = run_bass_kernel_spmd(nc, maps, core_ids=[0, 1, 2, 3])
    out = np.stack([np.asarray(r["outT"], dtype=np.float32).T for r in res.results], axis=0)
    return np.ascontiguousarray(out)


def stage_nsa(k, l):
    nc, P, ps = k.nc, k.P, k.ps
    SC = 128 ** -0.5
    with ExitStack() as st:
        sb = lambda name, shape, dt: st.enter_context(nc.sbuf_tensor(_uniq(name), list(shape), dt))
        qT = sb("n_qT", [128, 16, 512], BF16)
        kcr = sb("n_kcr", [128, T], BF16)
        vcr = sb("n_vcr", [128, T], BF16)
        ksT = sb("n_ksT", [128, T], BF16)
        kwT = sb("n_kwT", [128, T], BF16)
        vs = sb("n_vs", [128, 16, 132], BF16)
        vw = sb("n_vw", [128, 16, 132], BF16)
        wk = sb("n_wk", [128, 32, 128], BF16)
        wv = sb("n_wv", [128, 32, 128], BF16)
        peb = sb("n_peb", [128, 2, 32], BF16)
        cmpneg = sb("n_cmpneg", [128, T], BF16)
        ovl = sb("n_ovl", [128, 32], BF16)
        expand = sb("n_expand", [32, T], BF16)
        trineg = sb("n_trineg", [128, 512], BF16)
        farneg = sb("n_farneg", [128, 512], BF16)
        selvm = sb("n_selvm", [128, 16, 32], F32)
        selam = sb("n_selam", [128, 16, 32], F32)
        cosC = sb("n_cosC", [128, 128], F32)
        sinC = sb("n_sinC", [128, 128], F32)
        sg = sb("n_sg", [128, 16, 24], F32)
        bk = sb("n_bk", [128, 1], F32)
        kcf = sb("n_kcf", [128, 128], F32)
        kcb = sb("n_kcb", [128, 128], BF16)
        kt1 = sb("n_kt1", [128, 128], F32)
        kcT = sb("n_kcT", [128, 128], BF16)
        bvr = sb("n_bvr", [1, 128], BF16)
        vce = sb("n_vce", [128, 164], BF16)
        PT = [sb(f"n_PT{i}", [128, 17, 512], BF16) for i in range(2)]
        PTw = [sb(f"n_PTw{i}", [128, 5, 512], BF16) for i in range(2)]
        PTc = [sb(f"n_PTc{i}", [128, 512], BF16) for i in range(2)]
        sm = [sb(f"n_sm{i}", [128, 8, 4], F32) for i in range(2)]
        imp = sb("n_imp", [128, 32], F32)
        sc = sb("n_sc", [128, 32], F32)
        sc2 = sb("n_sc2", [128, 32], F32)
        m8 = sb("n_m8", [128, 16], F32)
        negm = sb("n_negm", [128, 32], BF16)
        selT4 = [sb(f"n_selT{i}", [32, 4, 128], BF16) for i in range(2)]
        ycomb = [sb(f"n_yc{i}", [128, 4, 128], F32) for i in range(2)]
        yb = sb("n_yb", [128, 512], BF16)
        ybT = [sb(f"n_ybT{i}", [128, 512], BF16) for i in range(2)]
        ld = lambda q, out, in_, key, rd=(): P.dma(q, lambda e: e.dma_start(out=out, in_=in_), reads=list(rd), writes=[key])
        C = k.consts
        ld("sp", cmpneg[:], C["cmpneg"], "n_cmpneg")
        ld("sp", ovl[:], C["ovl"], "n_ovl")
        ld("sp", expand[:], C["expand"], "n_expand")
        ld("sp", trineg[:], C["trineg"], "n_trineg")
        ld("sp", farneg[:], C["farneg"], "n_farneg")
        ld("sp", selvm[:], C["selvm"], "n_selvm")
        ld("sp", selam[:], C["selam"], "n_selam")
        ld("sp", cosC[:], C["cosC"], "n_cosC")
        ld("sp", sinC[:], C["sinC"], "n_sinC")
        ld("sp", sg[:], k.gates.rearrange("(tb p) n -> p tb n", p=128)[:, :, 8:32], "n_sg")
        P.act(lambda e: e.activation(out=sg[:], in_=sg[:], func=AF.Sigmoid), reads=["n_sg"], writes=["n_sg"])
        ld("pool", wk[:], k.cmpw[l, 0].rearrange("(l d) o -> d l o", d=128), "n_wk")
        ld("pool", wv[:], k.cmpw[l, 1].rearrange("(l d) o -> d l o", d=128), "n_wv")
        ld("pool", peb[:], k.peT[l].rearrange("k d l -> d k l"), "n_peb")
        nS = [0]

        def sbank():
            b = nS[0] % 2
            nS[0] += 1
            return b

        for g in range(2):
            ld("sp", qT[:], k.nqT[g].rearrange("d tb h t -> d tb (h t)"), "n_qT")
            ld("sp", kcr[:], k.ncT[g * 128:(g + 1) * 128, :], "n_kcr")
            ld("sp", vcr[:], k.ncT[256 + g * 128:256 + (g + 1) * 128, :], "n_vcr")
            ld("sp", ksT[:], k.nkT[g * 128:(g + 1) * 128, :], "n_ksT")
            ld("sp", kwT[:], k.nkT[256 + g * 128:256 + (g + 1) * 128, :], "n_kwT")
            nvv = k.nv_tm.rearrange("(kt p) d -> p kt d", p=128)
            ld("sp", vs[:, :, 0:128], nvv[:, :, g * 128:(g + 1) * 128], "n_vs")
            ld("sp", vw[:, :, 0:128], nvv[:, :, 256 + g * 128:256 + (g + 1) * 128], "n_vw")
            P.dve(lambda e: e.memset(vs[:, :, 128:129], 1.0), writes=["n_vs"])
            P.dve(lambda e: e.memset(vw[:, :, 128:129], 1.0), writes=["n_vw"])
            for li in range(32):
                P.pe(lambda e, li=li: e.matmul(ps[:, 0, 0:127], wk[:, li, :], kcr[:, li:li + 16 * 126 + 1:16],
                                               start=(li == 0), stop=(li == 31)),
                     reads=["n_wk", "n_kcr"], writes=ps_keys(0))
            for li in range(32):
                P.pe(lambda e, li=li: e.matmul(ps[:, 1, 0:1], wk[:, li, :], peb[:, 0, li:li + 1],
                                               start=(li == 0), stop=(li == 31)),
                     reads=["n_wk", "n_peb"], writes=ps_keys(1))
            P.act(lambda e: e.activation(out=bk[:], in_=ps[:, 1, 0:1], func=AF.Copy), reads=ps_keys(1), writes=["n_bk"])
            P.dve(lambda e: e.memset(kcf[:], 0.0), writes=["n_kcf"])
            P.act(lambda e: e.activation(out=kcf[:, 0:127], in_=ps[:, 0, 0:127], func=AF.Identity, bias=bk[:, 0:1], scale=1.0),
                  reads=ps_keys(0) + ["n_bk"], writes=["n_kcf"])
            P.act(lambda e: e.activation(out=kcb[:], in_=kcf[:], func=AF.Copy), reads=["n_kcf"], writes=["n_kcb"])
            P.pe(lambda e: e.matmul(ps[:, 2, 0:128], k.pswap[:], kcb[:], start=True, stop=True), reads=["pswap", "n_kcb"], writes=ps_keys(2))
            P.dve(lambda e: e.tensor_tensor(out=kt1[:], in0=ps[:, 2, 0:128], in1=sinC[:], op=ALU.mult), reads=ps_keys(2) + ["n_sinC"], writes=["n_kt1"])
            P.dve(lambda e: e.tensor_tensor(out=kcf[:], in0=kcf[:], in1=cosC[:], op=ALU.mult), reads=["n_kcf", "n_cosC"], writes=["n_kcf"])
            P.dve(lambda e: e.tensor_tensor(out=kcT[:], in0=kcf[:], in1=kt1[:], op=ALU.add), reads=["n_kcf", "n_kt1"], writes=["n_kcT"])
            for li in range(32):
                P.pe(lambda e, li=li: e.matmul(ps[0:1, 4, 0:128], peb[:, 1, li:li + 1], wv[:, li, :],
                                               start=(li == 0), stop=(li == 31)),
                     reads=["n_wv", "n_peb"], writes=ps_keys(4))
            P.act(lambda e: e.activation(out=bvr[:], in_=ps[0:1, 4, 0:128], func=AF.Copy), reads=ps_keys(4), writes=["n_bvr"])
            for li in range(32):
                P.pe(lambda e, li=li: e.matmul(ps[0:127, 3, 0:128], vcr[:, li:li + 16 * 126 + 1:16], wv[:, li, :],
                                               start=(li == 0), stop=False),
                     reads=["n_wv", "n_vcr"], writes=ps_keys(3))
            P.pe(lambda e: e.matmul(ps[0:127, 3, 0:128], k.ones_bf[0:1, 0:127], bvr[:], start=False, stop=True),
                 reads=["ones_bf", "n_bvr"], writes=ps_keys(3))
            P.dve(lambda e: e.memset(vce[:], 0.0), writes=["n_vce"])
            P.act(lambda e: e.activation(out=vce[0:127, 0:128], in_=ps[0:127, 3, 0:128], func=AF.Copy), reads=ps_keys(3), writes=["n_vce"])
            P.dve(lambda e: e.memset(vce[:, 128:129], 1.0), writes=["n_vce"])
            P.dve(lambda e: e.tensor_copy(out=vce[:, 129:161], in_=ovl[:]), reads=["n_ovl"], writes=["n_vce"])
            for tb in range(16):
                par = tb % 2
                qs = qT[:, tb, :]
                s_ = sm[par]
                Y = ycomb[par]
                sgv = sg[:, tb, g * 12:(g + 1) * 12].rearrange("p (h r) -> p h r", r=3)
                bS = sbank()
                P.pe(lambda e, bS=bS, qs=qs: e.matmul(ps[:, bS, :], kcT[:], qs, start=True, stop=False),
                     reads=["n_kcT", "n_qT"], writes=ps_keys(bS))
                P.pe(lambda e, bS=bS, tb=tb: e.matmul(ps[:, bS, :], k.ident[:],
                                                       cmpneg[:, tb * 128:(tb + 1) * 128].unsqueeze(1).to_broadcast([128, 4, 128]),
                                                       start=False, stop=True),
                     reads=["ident", "n_cmpneg"], writes=ps_keys(bS))
                P.act(lambda e, bS=bS, par=par: e.activation(out=PTc[par][:], in_=ps[:, bS, :], func=AF.Exp, scale=SC),
                      reads=ps_keys(bS), writes=[("n_PTc", par)])
                for h in range(4):
                    P.pe(lambda e, h=h, par=par: e.matmul(ps[:, 2 + h // 2, (h % 2) * 256:(h % 2) * 256 + 161],
                                                          PTc[par][:, h * 128:(h + 1) * 128], vce[:, 0:161], start=True, stop=True),
                         reads=[("n_PTc", par), "n_vce"], writes=ps_keys(2 + h // 2))
                oc = ps[:, 2:4, :].rearrange("p a (b x) -> p (a b) x", x=256)
                rkc = ps_keys(2, 2)
                smk = [("n_sm", par)]
                P.dve(lambda e, oc=oc, s_=s_: e.tensor_scalar(out=s_[:, 0, :], in0=oc[:, :, 128], scalar1=1e-30, scalar2=None, op0=ALU.max),
                      reads=rkc, writes=smk)
                P.dve(lambda e, s_=s_: e.reciprocal(out=s_[:, 0, :], in_=s_[:, 0, :]), reads=smk, writes=smk)
                P.dve(lambda e, oc=oc, s_=s_: e.tensor_scalar(out=imp[:], in0=oc[:, 0, 129:161], scalar1=s_[:, 0, 0:1], scalar2=None, op0=ALU.mult),
                      reads=rkc + smk, writes=["n_imp"])
                for h in range(1, 4):
                    P.dve(lambda e, oc=oc, s_=s_, h=h: e.scalar_tensor_tensor(out=imp[:], in0=oc[:, h, 129:161], scalar=s_[:, 0, h:h + 1],
                                                                              in1=imp[:], op0=ALU.mult, op1=ALU.add),
                          reads=rkc + smk + ["n_imp"], writes=["n_imp"])
                P.dve(lambda e, s_=s_, sgv=sgv: e.tensor_tensor(out=s_[:, 1, :], in0=s_[:, 0, :], in1=sgv[:, :, 0], op=ALU.mult),
                      reads=smk + ["n_sg"], writes=smk)
                P.dve(lambda e, oc=oc, s_=s_, Y=Y: e.tensor_tensor(out=Y[:], in0=oc[:, :, 0:128],
                                                                   in1=s_[:, 1, :].unsqueeze(2).to_broadcast([128, 4, 128]), op=ALU.mult),
                      reads=rkc + smk, writes=[("n_yc", par)])
                P.dve(lambda e, tb=tb: e.tensor_tensor(out=sc[:], in0=imp[:], in1=selvm[:, tb, :], op=ALU.mult),
                      reads=["n_imp", "n_selvm"], writes=["n_sc"])
                P.dve(lambda e, tb=tb: e.tensor_tensor(out=sc[:], in0=sc[:], in1=selam[:, tb, :], op=ALU.add),
                      reads=["n_sc", "n_selam"], writes=["n_sc"])
                P.dve(lambda e: e.max(out=m8[:, 0:8], in_=sc[:]), reads=["n_sc"], writes=["n_m8"])
                P.dve(lambda e: e.match_replace(out=sc2[:], in_to_replace=m8[:, 0:8], in_values=sc[:], imm_value=-3e4),
                      reads=["n_sc", "n_m8"], writes=["n_sc2"])
                P.dve(lambda e: e.max(out=m8[:, 8:16], in_=sc2[:]), reads=["n_sc2"], writes=["n_m8"])
                P.dve(lambda e: e.tensor_scalar(out=sc2[:], in0=sc[:], scalar1=m8[:, 15:16], scalar2=None, op0=ALU.is_lt),
                      reads=["n_sc", "n_m8"], writes=["n_sc2"])
                P.dve(lambda e: e.tensor_scalar(out=negm[:], in0=sc2[:], scalar1=NEGM, scalar2=None, op0=ALU.mult),
                      reads=["n_sc2"], writes=["n_negm"])
                pTs = ps[:, 3, :].bitcast(BF16)[0:32, 896:1024]
                P.pe(lambda e, pTs=pTs: e.transpose(pTs, negm[:], k.ident[:]), reads=["n_negm", "ident"], writes=ps_keys(3))
                P.act(lambda e, pTs=pTs, par=par: e.activation(out=selT4[par][:], in_=pTs.unsqueeze(1).to_broadcast([32, 4, 128]), func=AF.Copy),
                      reads=ps_keys(3), writes=[("n_selT", par)])
                selr = selT4[par][:].rearrange("p h t -> p (h t)")
                for kt in range(tb + 1):
                    bS = sbank()
                    ks = slice(kt * 128, (kt + 1) * 128)
                    P.pe(lambda e, bS=bS, ks=ks, qs=qs: e.matmul(ps[:, bS, :], ksT[:, ks], qs, start=True, stop=False),
                         reads=["n_ksT", "n_qT"], writes=ps_keys(bS))
                    P.pe(lambda e, bS=bS, ks=ks, selr=selr, kt=kt, tb=tb: e.matmul(ps[:, bS, :], expand[:, ks], selr, start=False, stop=(kt != tb)),
                         reads=["n_expand", ("n_selT", par)], writes=ps_keys(bS))
                    if kt == tb:
                        P.pe(lambda e, bS=bS: e.matmul(ps[:, bS, :], k.ident[:], trineg[:], start=False, stop=True),
                             reads=["ident", "n_trineg"], writes=ps_keys(bS))
                    P.act(lambda e, bS=bS, par=par, kt=kt: e.activation(out=PT[par][:, kt, :], in_=ps[:, bS, :], func=AF.Exp, scale=SC),
                          reads=ps_keys(bS), writes=[("n_PT", par)])
                for h in range(4):
                    for kt in range(tb + 1):
                        P.pe(lambda e, h=h, kt=kt, par=par, tb=tb: e.matmul(
                            ps[:, 4 + h // 2, (h % 2) * 256:(h % 2) * 256 + 129],
                            PT[par][:, kt, h * 128:(h + 1) * 128], vs[:, kt, 0:129], start=(kt == 0), stop=(kt == tb)),
                            reads=[("n_PT", par), "n_vs"], writes=ps_keys(4 + h // 2))
                kts = list(range(max(0, tb - 4), tb + 1))
                for j, kt in enumerate(kts):
                    bS = sbank()
                    ks = slice(kt * 128, (kt + 1) * 128)
                    extra = []
                    if kt == tb:
                        extra.append((trineg, "n_trineg"))
                    if kt == tb - 4:
                        extra.append((farneg, "n_farneg"))
                    P.pe(lambda e, bS=bS, ks=ks, qs=qs, last=(not extra): e.matmul(ps[:, bS, :], kwT[:, ks], qs, start=True, stop=last),
                         reads=["n_kwT", "n_qT"], writes=ps_keys(bS))
                    for xi, (mt, mk_) in enumerate(extra):
                        P.pe(lambda e, bS=bS, mt=mt, last=(xi == len(extra) - 1): e.matmul(ps[:, bS, :], k.ident[:], mt[:], start=False, stop=last),
                             reads=["ident", mk_], writes=ps_keys(bS))
                    P.act(lambda e, bS=bS, par=par, j=j: e.activation(out=PTw[par][:, j, :], in_=ps[:, bS, :], func=AF.Exp, scale=SC),
                          reads=ps_keys(bS), writes=[("n_PTw", par)])
                for h in range(4):
                    for j, kt in enumerate(kts):
                        P.pe(lambda e, h=h, kt=kt, j=j, par=par, n=len(kts): e.matmul(
                            ps[:, 6 + h // 2, (h % 2) * 256:(h % 2) * 256 + 129],
                            PTw[par][:, j, h * 128:(h + 1) * 128], vw[:, kt, 0:129], start=(j == 0), stop=(j == n - 1)),
                            reads=[("n_PTw", par), "n_vw"], writes=ps_keys(6 + h // 2))
                for br, b0 in ((1, 4), (2, 6)):
                    ob_ = ps[:, b0:b0 + 2, :].rearrange("p a (b x) -> p (a b) x", x=256)
                    rk = ps_keys(b0, 2)
                    P.dve(lambda e, ob_=ob_, s_=s_, br=br: e.tensor_scalar(out=s_[:, 2 * br, :], in0=ob_[:, :, 128], scalar1=1e-30, scalar2=None, op0=ALU.max),
                          reads=rk, writes=smk)
                    P.dve(lambda e, s_=s_, br=br: e.reciprocal(out=s_[:, 2 * br, :], in_=s_[:, 2 * br, :]), reads=smk, writes=smk)
                    P.dve(lambda e, s_=s_, br=br, sgv=sgv: e.tensor_tensor(out=s_[:, 2 * br + 1, :], in0=s_[:, 2 * br, :], in1=sgv[:, :, br], op=ALU.mult),
                          reads=smk + ["n_sg"], writes=smk)
                    for h in range(4):
                        P.dve(lambda e, ob_=ob_, s_=s_, br=br, h=h, Y=Y: e.scalar_tensor_tensor(
                            out=Y[:, h, :], in0=ob_[:, h, 0:128], scalar=s_[:, 2 * br + 1, h:h + 1], in1=Y[:, h, :],
                            op0=ALU.mult, op1=ALU.add),
                            reads=rk + smk + [("n_yc", par)], writes=[("n_yc", par)])
                P.act(lambda e, Y=Y: e.activation(out=yb[:], in_=Y[:].rearrange("p h x -> p (h x)"), func=AF.Copy),
                      reads=[("n_yc", par)], writes=["n_yb"])
                bT = sbank()
                pT = ps[:, bT, :].bitcast(BF16)
                for h in range(4):
                    P.pe(lambda e, h=h, pT=pT: e.transpose(pT[:, h * 128:(h + 1) * 128], yb[:, h * 128:(h + 1) * 128], k.ident[:]),
                         reads=["n_yb", "ident"], writes=ps_keys(bT))
                P.act(lambda e, pT=pT, par=par: e.activation(out=ybT[par][:], in_=pT[:, 0:512], func=AF.Copy),
                      reads=ps_keys(bT), writes=[("n_ybT", par)])
                r0 = 1024 + g * 512
                P.dma("sp", lambda e, par=par, tb=tb, r0=r0: e.dma_start(
                    out=k.yT[r0:r0 + 512, tb * 128:(tb + 1) * 128].rearrange("(h d) t -> d h t", d=128),
                    in_=ybT[par][:].rearrange("p (h t) -> p h t", t=128)),
                    reads=[("n_ybT", par)], writes=[("yT", "c", g, tb)])
    P.barrier()
```

```python
import numpy as np
from contextlib import ExitStack
import concourse.bass as bass
import concourse.mybir as mybir
from concourse.bass_utils import run_bass_kernel_spmd

F32 = mybir.dt.float32
BF16 = mybir.dt.bfloat16
AF = mybir.ActivationFunctionType
ALU = mybir.AluOpType
AX = mybir.AxisListType

T = 2048
D = 2048
L = 4
NIN = 5664
DFF = 5632
EPS = 1e-6
NEGM = -30000.0
EPOCH = 30000


class Op:
    __slots__ = ("eng", "fn", "reads", "writes", "dma", "deps", "sig", "idx")

    def __init__(self, eng, fn, reads, writes, dma):
        self.eng, self.fn, self.reads, self.writes, self.dma = eng, fn, reads, writes, dma
        self.deps = ()
        self.sig = None


class Prog:
    ENGS = ("pe", "act", "dve", "pool", "sp")

    def __init__(self, nc, n_dma_sems=16):
        self.nc = nc
        self.ops = []
        self.n_dma_sems = n_dma_sems

    def _add(self, eng, fn, reads, writes, dma):
        reads, writes = list(reads), list(writes)
        for r in reads:
            if isinstance(r, tuple) and r and r[0] == "ps" and r not in writes:
                writes.append(r)
        o = Op(eng, fn, tuple(reads), tuple(writes), dma)
        o.idx = len(self.ops)
        self.ops.append(o)
        return o

    def pe(self, fn, reads=(), writes=()):
        return self._add("pe", fn, reads, writes, False)

    def act(self, fn, reads=(), writes=()):
        return self._add("act", fn, reads, writes, False)

    def dve(self, fn, reads=(), writes=()):
        return self._add("dve", fn, reads, writes, False)

    def pool(self, fn, reads=(), writes=()):
        return self._add("pool", fn, reads, writes, False)

    def dma(self, q, fn, reads=(), writes=()):
        return self._add(q, fn, reads, writes, True)

    def barrier(self):
        return self._add(None, None, (), (), False)

    def emit(self, stack):
        nc = self.nc
        ops = self.ops
        engs = {"pe": nc.tensor, "act": nc.scalar, "dve": nc.vector, "pool": nc.gpsimd, "sp": nc.sync}
        last_w, readers, needed = {}, {}, set()
        last_compute = {}
        dmas_since = []
        pending = {e: set() for e in engs}
        for o in ops:
            if o.eng is None:
                bd = set(last_compute.values()) | set(dmas_since)
                for e in engs:
                    pending[e] |= bd
                dmas_since = []
                continue
            deps = set()
            for r in o.reads:
                j = last_w.get(r)
                if j is not None:
                    deps.add(j)
            for w in o.writes:
                j = last_w.get(w)
                if j is not None:
                    deps.add(j)
                deps.update(readers.get(w, ()))
            if pending[o.eng]:
                deps |= pending[o.eng]
                pending[o.eng] = set()
            deps.discard(o.idx)
            fd = []
            for j in deps:
                p = ops[j]
                if p.eng == o.eng == "pe":
                    continue
                fd.append(j)
                needed.add(j)
            o.deps = fd
            for w in o.writes:
                last_w[w] = o.idx
                readers[w] = []
            for r in o.reads:
                if r not in o.writes:
                    readers.setdefault(r, []).append(o.idx)
            if o.dma:
                dmas_since.append(o.idx)
            else:
                last_compute[o.eng] = o.idx
        cnt = {e: 0 for e in engs}
        eng_sems = {e: [] for e in engs}
        dma_sems, dma_rr, dma_uses = {}, {}, {}

        def get_eng_sem(e, ep):
            while len(eng_sems[e]) <= ep:
                eng_sems[e].append(stack.enter_context(nc.semaphore(f"c_{e}_{len(eng_sems[e])}")))
            return eng_sems[e][ep]

        reuse_wait = {}
        for o in ops:
            if o.eng is None:
                continue
            if o.dma:
                q = o.eng
                if q not in dma_sems:
                    dma_sems[q] = [stack.enter_context(nc.semaphore(f"d_{q}_{k}")) for k in range(self.n_dma_sems)]
                    dma_rr[q] = 0
                k = dma_rr[q]
                dma_rr[q] = (k + 1) % self.n_dma_sems
                u = dma_uses.get((q, k), 0) + 1
                dma_uses[(q, k)] = u
                o.sig = (dma_sems[q][k], 16 * u, 16)
                if u > 1:
                    reuse_wait[o.idx] = (dma_sems[q][k], 16 * (u - 1))
            elif o.idx in needed:
                c = cnt[o.eng]
                cnt[o.eng] = c + 1
                o.sig = (get_eng_sem(o.eng, c // EPOCH), c % EPOCH + 1, 1)
        waited = {e: {} for e in engs}
        nwaits = 0
        for o in ops:
            if o.eng is None:
                continue
            eng = engs[o.eng]
            wl = {}
            for j in o.deps:
                s, v, _ = ops[j].sig
                if wl.get(s.num, (None, 0))[1] < v:
                    wl[s.num] = (s, v)
            if o.idx in reuse_wait:
                s, v = reuse_wait[o.idx]
                if wl.get(s.num, (None, 0))[1] < v:
                    wl[s.num] = (s, v)
            for num, (s, v) in wl.items():
                if waited[o.eng].get(num, 0) >= v:
                    continue
                eng.wait_ge(s, v)
                waited[o.eng][num] = v
                nwaits += 1
            ins = o.fn(eng)
            if o.sig is not None:
                s, v, inc = o.sig
                ins.then_inc(s, inc)
        for q, sems in dma_sems.items():
            for k, s in enumerate(sems):
                u = dma_uses.get((q, k), 0)
                if u:
                    nc.sync.wait_ge(s, 16 * u)
        for e, sems in eng_sems.items():
            c = cnt[e]
            for ep, s in enumerate(sems):
                v = min(EPOCH, c - ep * EPOCH)
                if v > 0:
                    nc.sync.wait_ge(s, v)
        self.stats = dict(n_ops=len(ops), n_waits=nwaits, sig=dict(cnt))


_ORIG = dict(mq=(0, 512), mk=(512, 1024), mv=(1024, 1536), mo=(1536, 2048), mi=(2048, 2052), mf=(2052, 2056),
             ca=(2056, 2568), cg=(2568, 3080), nq=(3080, 4104), nkc=(4104, 4360), nvc=(4360, 4616),
             nks=(4616, 4872), nvs=(4872, 5128), nkw=(5128, 5384), nvw=(5384, 5640), ngt=(5640, 5664))
_ORDER = ["mq", "mk", "mv", "mo", "ca", "cg", "nq", "nkc", "nvc", "nks", "nkw", "nvs", "nvw", "mi", "mf", "ngt"]
PERM = np.concatenate([np.arange(*_ORIG[k]) for k in _ORDER])


def host_consts():
    import ml_dtypes
    bf = ml_dtypes.bfloat16
    c = {}
    half = 64
    inv = (np.float32(10000.0) ** (-np.arange(half, dtype=np.float32) / np.float32(half))).astype(np.float32)

    def tabs(pos):
        ang = pos.astype(np.float32)[:, None] * inv[None, :]
        cos, sin = np.cos(ang).astype(np.float32), np.sin(ang).astype(np.float32)
        cosT = np.concatenate([cos, cos], 1).T
        sinT = np.concatenate([-sin, sin], 1).T
        return np.ascontiguousarray(cosT), np.ascontiguousarray(sinT)

    c["cosT"], c["sinT"] = tabs(np.arange(T))
    cend = np.arange(127) * 16 + 31
    cc, ss = tabs(cend)
    c["cosC"] = np.zeros((128, 128), np.float32); c["cosC"][:, :127] = cc
    c["sinC"] = np.zeros((128, 128), np.float32); c["sinC"][:, :127] = ss
    ident = np.eye(128, dtype=np.float32)
    c["ident"] = ident.astype(bf)
    pswap = np.zeros((128, 128), np.float32)
    for d in range(64):
        pswap[d, d + 64] = 1.0
        pswap[d + 64, d] = 1.0
    c["pswap"] = pswap.astype(bf)
    a = np.arange(128)
    tri = np.where(a[:, None] <= a[None, :], 0.0, NEGM).astype(np.float32)
    far = np.where(a[:, None] > a[None, :], 0.0, NEGM).astype(np.float32)
    c["trineg"] = np.tile(tri, (1, 4)).astype(bf)
    c["farneg"] = np.tile(far, (1, 4)).astype(bf)
    c["tri01"] = np.tile((a[:, None] <= a[None, :]).astype(np.float32), (1, 4))
    c["triU"] = (a[:, None] <= a[None, :]).astype(np.float32)
    n = np.arange(128)
    cm = np.where((n[:, None] * 16 + 31 <= np.arange(T)[None, :]) & (n[:, None] < 127), 0.0, NEGM).astype(np.float32)
    c["cmpneg"] = cm.astype(bf)
    nn = np.arange(127)
    bstart = np.arange(32) * 64
    ov = ((nn[:, None] * 16 <= bstart[None, :] + 63) & (nn[:, None] * 16 + 31 >= bstart[None, :])).astype(np.float32)
    ovp = np.zeros((128, 32), np.float32); ovp[:127] = ov
    c["ovl"] = ovp.astype(bf)
    key = np.arange(T)
    c["expand"] = (key[None, :] // 64 == np.arange(32)[:, None]).astype(np.float32).astype(bf)
    t = np.arange(T)
    j = np.arange(32)[None, :]
    qblk = t[:, None] // 64
    forced = (j == 0) | (j == qblk) | (j == qblk - 1)
    valid = j * 64 <= t[:, None]
    vm = (valid & ~forced).astype(np.float32)
    am = np.where(forced, 1e4, np.where(valid, 0.0, -1e4)).astype(np.float32)
    c["selvm"] = np.ascontiguousarray(vm.reshape(16, 128, 32).transpose(1, 0, 2))
    c["selam"] = np.ascontiguousarray(am.reshape(16, 128, 32).transpose(1, 0, 2))
    return c


CONST_SPECS = dict(cosT=([128, T], F32), sinT=([128, T], F32), cosC=([128, 128], F32), sinC=([128, 128], F32),
                   ident=([128, 128], BF16), pswap=([128, 128], BF16), trineg=([128, 512], BF16),
                   farneg=([128, 512], BF16), tri01=([128, 512], F32), triU=([128, 128], F32),
                   cmpneg=([128, T], BF16), ovl=([128, 32], BF16), expand=([32, T], BF16),
                   selvm=([128, 16, 32], F32), selam=([128, 16, 32], F32))


class K:
    pass


_UID = [0]


def _uniq(name):
    _UID[0] += 1
    return f"{name}_u{_UID[0]}"


def build(n_layers=L, stages=("s1", "conf", "mlstm", "nsa", "wout", "ffn"), debug=(), LW=L):
    nc = bass.Bass("TRN2", target_bir_lowering=False)
    k = K()
    k.nc = nc
    k.debug = debug
    k.stages = stages

    def din(name, shape, dt):
        return nc.dram_tensor(name, list(shape), dt, kind="ExternalInput").ap()

    def dscr(name, shape, dt):
        kind = "ExternalOutput" if name in debug else "Internal"
        return nc.dram_tensor(name, list(shape), dt, kind=kind).ap()

    k.xT_in = din("xT", [D, T], F32)
    k.w_in = din("w_in", [LW, D, NIN], F32)
    k.w_out = din("w_out", [LW, D, D], F32)
    k.w_up = din("w_up", [LW, D, 2 * DFF], F32)
    k.w_down = din("w_down", [LW, DFF, D], F32)
    k.nw1 = din("nw1", [L, 128, 16], F32)
    k.nw2 = din("nw2", [L, 128, 16], F32)
    k.nwf = din("nwf", [128, 16], F32)
    k.gbias = din("gbias", [L, 8], F32)
    k.mnw = din("mnw", [L, 512], F32)
    k.cdw = din("cdw", [L, 4, 128, 31], F32)
    k.cvec = din("cvec", [L, 4, 128, 3], F32)
    k.peT = din("peT", [L, 2, 128, 32], F32)
    k.cmpw = din("cmpw", [L, 2, 4096, 128], F32)
    k.fdw = din("fdw", [L, 128, 88, 3], F32)
    k.consts = {n: din("c_" + n, s, dt) for n, (s, dt) in CONST_SPECS.items()}
    k.outT = nc.dram_tensor("outT", [D, T], F32, kind="ExternalOutput").ap()

    k.mqT = dscr("mqT", [512, T], BF16)
    k.mkT = dscr("mkT", [512, T], BF16)
    k.mk_tm = dscr("mk_tm", [T, 512], BF16)
    k.mv_tm = dscr("mv_tm", [T, 512], BF16)
    k.mo_tm = dscr("mo_tm", [T, 512], BF16)
    k.nqT = dscr("nqT", [2, 128, 16, 4, 128], BF16)
    k.ncT = dscr("ncT", [512, T], BF16)
    k.nkT = dscr("nkT", [512, T], BF16)
    k.nv_tm = dscr("nv_tm", [T, 512], BF16)
    k.gates = dscr("gates", [T, 32], F32)
    k.yT = dscr("yT", [D, T], BF16)
    k.xm = dscr("xm", [D, T], F32)
    k.xo = dscr("xo", [D, T], F32)
    k.aT = dscr("aT", [DFF, T], BF16)

    with ExitStack() as st:
        P = Prog(nc)
        k.P = P
        k.ps = st.enter_context(nc.psum_tensor("ps", [128, 8, 512], F32))
        sb = lambda name, shape, dt: st.enter_context(nc.sbuf_tensor(_uniq(name), list(shape), dt))
        k.ident = sb("ident", [128, 128], BF16)
        k.pswap = sb("pswap", [128, 128], BF16)
        k.ones_bf = sb("ones_bf", [128, 128], BF16)
        k.ones_f = sb("ones_f", [128, 128], F32)
        P.dma("sp", lambda e: e.dma_start(out=k.ident[:], in_=k.consts["ident"]), writes=["ident"])
        P.dma("sp", lambda e: e.dma_start(out=k.pswap[:], in_=k.consts["pswap"]), writes=["pswap"])
        P.dve(lambda e: e.memset(k.ones_bf[:], 1.0), writes=["ones_bf"])
        P.dve(lambda e: e.memset(k.ones_f[:], 1.0), writes=["ones_f"])
        xcur, xkey = k.xT_in, "xin"
        for l in range(n_layers):
            if "s1" in stages:
                stage_s1(k, l, xcur, xkey)
            P.barrier()
            if "mlstm" in stages:
                stage_mlstm(k, l)
            if "nsa" in stages:
                stage_nsa(k, l)
            if "wout" in stages:
                stage_wout(k, l, xcur, xkey)
            if "ffn" in stages:
                stage_ffn(k, l)
                xcur, xkey = k.xo, "xo"
        if "final" in stages:
            stage_final(k)
        P.emit(st)
        k.stats = P.stats
    return nc, k


def ps_keys(b0, n=1):
    return [("ps", b) for b in range(b0, b0 + n)]


def rms_to_hT(k, st, l, x_src, xkey, nw_dram, hT):
    nc, P, ps = k.nc, k.P, k.ps
    sb = lambda name, shape, dt: st.enter_context(nc.sbuf_tensor(_uniq(name), list(shape), dt))
    xin = [sb(f"rn_xin{i}", [128, T], F32) for i in range(2)]
    sq = [sb(f"rn_sq{i}", [128, T], BF16) for i in range(2)]
    rstd = sb("rn_rstd", [128, T], F32)
    nw = sb("rn_nw", [128, 16], F32)
    P.dma("sp", lambda e: e.dma_start(out=nw[:], in_=nw_dram), writes=["rn_nw"])
    for kc in range(16):
        i = kc % 2
        P.dma("sp", lambda e, kc=kc, i=i: e.dma_start(out=xin[i][:], in_=x_src[kc * 128:(kc + 1) * 128, :]),
              reads=[(xkey, kc)], writes=[("rn_xin", i)])
        P.act(lambda e, i=i: e.activation(out=sq[i][:], in_=xin[i][:], func=AF.Square),
              reads=[("rn_xin", i)], writes=[("rn_sq", i)])
        for q in range(4):
            P.pe(lambda e, kc=kc, i=i, q=q: e.matmul(ps[:, q, :], k.ones_bf[:], sq[i][:, q * 512:(q + 1) * 512],
                                                      start=(kc == 0), stop=(kc == 15)),
                 reads=[("rn_sq", i), "ones_bf"], writes=ps_keys(q))
    for q in range(4):
        P.act(lambda e, q=q: e.activation(out=rstd[:, q * 512:(q + 1) * 512], in_=ps[:, q, :], func=AF.Sqrt,
                                          bias=EPS, scale=1.0 / D),
              reads=ps_keys(q), writes=["rn_rstd"])
    P.dve(lambda e: e.reciprocal(out=rstd[:], in_=rstd[:]), reads=["rn_rstd"], writes=["rn_rstd"])
    for kc in range(16):
        i = kc % 2
        P.dma("sp", lambda e, kc=kc, i=i: e.dma_start(out=xin[i][:], in_=x_src[kc * 128:(kc + 1) * 128, :]),
              reads=[(xkey, kc)], writes=[("rn_xin", i)])
        P.dve(lambda e, kc=kc, i=i: e.scalar_tensor_tensor(out=hT[:, kc, :], in0=xin[i][:], scalar=nw[:, kc:kc + 1],
                                                           in1=rstd[:], op0=ALU.mult, op1=ALU.mult),
              reads=[("rn_xin", i), "rn_rstd", "rn_nw"], writes=["hT"])


def stage_s1(k, l, x_src, xkey):
    nc, P, ps = k.nc, k.P, k.ps
    with ExitStack() as st:
        sb = lambda name, shape, dt: st.enter_context(nc.sbuf_tensor(_uniq(name), list(shape), dt))
        hT = sb("hT", [128, 16, T], BF16)
        with ExitStack() as st2:
            rms_to_hT(k, st2, l, x_src, xkey, k.nw1[l], hT)
        P.barrier()
        wb = [sb(f"wb{i}", [128, 16, 512], BF16) for i in range(2)]
        cf = None
        if "conf" in k.stages:
            st_cf = ExitStack()
            cf = conf_proj_conv(k, l, hT, wb, st_cf)
        st_in = ExitStack()
        sb_outer = sb
        sb = lambda name, shape, dt: st_in.enter_context(nc.sbuf_tensor(_uniq(name), list(shape), dt))
        cosT = sb("cosT", [128, T], F32)
        sinT = sb("sinT", [128, T], F32)
        P.dma("sp", lambda e: e.dma_start(out=cosT[:], in_=k.consts["cosT"]), writes=["cosT"])
        P.dma("sp", lambda e: e.dma_start(out=sinT[:], in_=k.consts["sinT"]), writes=["sinT"])
        ob = [sb(f"ob{i}", [128, 1024], BF16) for i in range(4)]
        qb = [sb(f"qb{i}", [128, 1024], BF16) for i in range(2)]
        t1 = [sb(f"t1_{i}", [128, 1024], F32) for i in range(2)]
        t2 = [sb(f"t2_{i}", [128, 1024], F32) for i in range(2)]
        og = [sb(f"og{i}", [128, 32], F32) for i in range(2)]
        wv = k.w_in[l].rearrange("(kc p) n -> p kc n", p=128)
        groups = [
            (0, 512, "fm", k.mqT), (512, 512, "fm+tm", (k.mkT, k.mk_tm)), (1024, 512, "tm", k.mv_tm),
            (1536, 512, "tm", k.mo_tm), (3072, 512, "ropeq", 0), (3584, 512, "ropeq", 1),
            (4096, 512, "fm", k.ncT), (4608, 512, "rope", k.nkT), (5120, 512, "tm", k.nv_tm),
            (5632, 32, "gates", k.gates),
        ]
        cnt = dict(ob=0, rope=0, fm=0, tm=0, og=0)
        deferred = []

        def flush_deferred():
            while deferred:
                deferred.pop(0)()

        def fm_chunk(gi, c, kind, dest):
            for half in range(2):
                b0 = (cnt["fm"] % 3) * 2
                cnt["fm"] += 1
                for kc in range(16):
                    for q in range(2):
                        P.pe(lambda e, kc=kc, q=q, b0=b0, half=half: e.matmul(
                            ps[:, b0 + q, :], wb[gi % 2][:, kc, c * 128:(c + 1) * 128],
                            hT[:, kc, half * 1024 + q * 512: half * 1024 + (q + 1) * 512],
                            start=(kc == 0), stop=(kc == 15)),
                            reads=[("wb", gi % 2), "hT"], writes=[("ps", b0 + q)])
                flush_deferred()
                psv = ps[:, b0:b0 + 2, :].rearrange("p a b -> p (a b)")
                cols = slice(half * 1024, (half + 1) * 1024)
                oi = cnt["ob"] % 4
                cnt["ob"] += 1
                if kind == "fm":
                    P.act(lambda e, psv=psv, oi=oi: e.activation(out=ob[oi][:], in_=psv, func=AF.Copy),
                          reads=ps_keys(b0, 2), writes=[("ob", oi)])
                    P.dma("sp", lambda e, oi=oi, cols=cols: e.dma_start(out=dest[c * 128:(c + 1) * 128, cols], in_=ob[oi][:]),
                          reads=[("ob", oi)], writes=[("scr", id(dest), c, half)])
                else:
                    ri = cnt["rope"] % 2
                    cnt["rope"] += 1
                    P.act(lambda e, psv=psv, ri=ri: e.activation(out=qb[ri][:], in_=psv, func=AF.Copy),
                          reads=ps_keys(b0, 2), writes=[("qb", ri)])
                    for q in range(2):
                        P.dve(lambda e, q=q, b0=b0, ri=ri, half=half: e.tensor_tensor(
                            out=t1[ri][:, q * 512:(q + 1) * 512], in0=ps[:, b0 + q, :],
                            in1=cosT[:, half * 1024 + q * 512: half * 1024 + (q + 1) * 512], op=ALU.mult),
                            reads=ps_keys(b0 + q) + ["cosT", ("qb", ri)], writes=[("t1", ri)])

                    def rot(ri=ri, cols=cols, oi=oi, half=half):
                        for q in range(2):
                            P.pe(lambda e, q=q: e.matmul(ps[:, 6 + q, :], k.pswap[:], qb[ri][:, q * 512:(q + 1) * 512],
                                                         start=True, stop=True),
                                 reads=[("qb", ri), "pswap"], writes=[("ps", 6 + q)])
                        for q in range(2):
                            P.dve(lambda e, q=q: e.tensor_tensor(
                                out=t2[ri][:, q * 512:(q + 1) * 512], in0=ps[:, 6 + q, :],
                                in1=sinT[:, half * 1024 + q * 512: half * 1024 + (q + 1) * 512], op=ALU.mult),
                                reads=ps_keys(6 + q) + ["sinT"], writes=[("t2", ri)])
                        P.dve(lambda e: e.tensor_tensor(out=ob[oi][:], in0=t1[ri][:], in1=t2[ri][:], op=ALU.add),
                               reads=[("t1", ri), ("t2", ri)], writes=[("ob", oi)])
                        if kind == "ropeq":
                            g = dest
                            dv = k.nqT[g, :, half * 8:(half + 1) * 8, c, :]
                            P.dma("sp", lambda e: e.dma_start(out=dv, in_=ob[oi][:].rearrange("p (a b) -> p a b", b=128)),
                                  reads=[("ob", oi)], writes=[("scr", "nqT", g, c, half)])
                        else:
                            P.dma("sp", lambda e: e.dma_start(out=dest[c * 128:(c + 1) * 128, cols], in_=ob[oi][:]),
                                  reads=[("ob", oi)], writes=[("scr", id(dest), c, half)])
                    deferred.append(rot)

        def tm_group(gi, ncols, dest, gates=False):
            for tt in range(16):
                b0 = cnt["tm"] % 6
                cnt["tm"] += 1
                for kc in range(16):
                    P.pe(lambda e, kc=kc, b0=b0, tt=tt: e.matmul(
                        ps[:, b0, 0:ncols], hT[:, kc, tt * 128:(tt + 1) * 128], wb[gi % 2][:, kc, 0:ncols],
                        start=(kc == 0), stop=(kc == 15)),
                        reads=[("wb", gi % 2), "hT"], writes=[("ps", b0)])
                flush_deferred()
                if gates:
                    oi = cnt["og"] % 2
                    cnt["og"] += 1
                    P.act(lambda e, b0=b0, oi=oi: e.activation(out=og[oi][:], in_=ps[:, b0, 0:32], func=AF.Copy),
                          reads=ps_keys(b0), writes=[("og", oi)])
                    P.dma("sp", lambda e, oi=oi, tt=tt: e.dma_start(out=dest[tt * 128:(tt + 1) * 128, :], in_=og[oi][:]),
                          reads=[("og", oi)], writes=[("scr", "gates", tt)])
                else:
                    oi = cnt["ob"] % 4
                    cnt["ob"] += 1
                    P.act(lambda e, b0=b0, oi=oi: e.activation(out=ob[oi][:, 0:512], in_=ps[:, b0, :], func=AF.Copy),
                          reads=ps_keys(b0), writes=[("ob", oi)])
                    P.dma("sp", lambda e, oi=oi, tt=tt: e.dma_start(out=dest[tt * 128:(tt + 1) * 128, :], in_=ob[oi][:, 0:512]),
                          reads=[("ob", oi)], writes=[("scr", id(dest), tt)])

        for gi, (c0, ncols, kind, dest) in enumerate(groups):
            P.dma("pool", lambda e, gi=gi, c0=c0, ncols=ncols: e.dma_start(out=wb[gi % 2][:, :, 0:ncols], in_=wv[:, :, c0:c0 + ncols]),
                  writes=[("wb", gi % 2)])
            if kind == "fm":
                for c in range(4):
                    fm_chunk(gi, c, "fm", dest)
            elif kind == "fm+tm":
                for c in range(4):
                    fm_chunk(gi, c, "fm", dest[0])
                tm_group(gi, 512, dest[1])
            elif kind == "tm":
                tm_group(gi, 512, dest)
            elif kind in ("ropeq", "rope"):
                for c in range(4):
                    fm_chunk(gi, c, kind, dest)
            elif kind == "gates":
                tm_group(gi, 32, dest, gates=True)
        flush_deferred()
        P.barrier()
        st_in.close()
        if cf is not None:
            conf_ln(k, l, cf)
            P.barrier()
            st_cf.close()
    P.barrier()


def conf_proj_conv(k, l, hT, wb, st):
    nc, P, ps = k.nc, k.P, k.ps
    wv = k.w_in[l].rearrange("(kc p) n -> p kc n", p=128)
    sb = lambda name, shape, dt: st.enter_context(nc.sbuf_tensor(_uniq(name), list(shape), dt))
    cf = K()
    cf.cdw = cdw = sb("cdw_s", [128, 4, 31], F32)
    cf.cvec = cvec = sb("cvec_s", [128, 4, 3], F32)
    u = [sb(f"cu{i}", [128, T + 30], F32) for i in range(3)]
    sg = sb("csg", [128, T], F32)
    cf.v = v = sb("cv", [128, 4, T], F32)
    P.dma("sp", lambda e: e.dma_start(out=cdw[:], in_=k.cdw[l].rearrange("j p t -> p j t")), writes=["cdw"])
    P.dma("sp", lambda e: e.dma_start(out=cvec[:], in_=k.cvec[l].rearrange("j p t -> p j t")), writes=["cvec"])
    for i in range(3):
        P.dve(lambda e, i=i: e.memset(u[i][:, 0:30], 0.0), writes=[("cu", i)])
    P.dma("pool", lambda e: e.dma_start(out=wb[0][:], in_=wv[:, :, 2048:2560]), writes=[("wb", 0)])
    P.dma("pool", lambda e: e.dma_start(out=wb[1][:], in_=wv[:, :, 2560:3072]), writes=[("wb", 1)])
    n = 0
    for j in range(4):
        ui = j % 3
        for half in range(2):
            s_ = n % 2
            n += 1
            for wi, b0 in ((0, 4 * s_), (1, 4 * s_ + 2)):
                for kc in range(16):
                    for q in range(2):
                        P.pe(lambda e, kc=kc, q=q, b0=b0, wi=wi, half=half, j=j: e.matmul(
                            ps[:, b0 + q, :], wb[wi][:, kc, j * 128:(j + 1) * 128],
                            hT[:, kc, half * 1024 + q * 512: half * 1024 + (q + 1) * 512],
                            start=(kc == 0), stop=(kc == 15)),
                            reads=[("wb", wi), "hT"], writes=[("ps", b0 + q)])
            pa = ps[:, 4 * s_:4 * s_ + 2, :].rearrange("p a b -> p (a b)")
            pg = ps[:, 4 * s_ + 2:4 * s_ + 4, :].rearrange("p a b -> p (a b)")
            cols = slice(half * 1024, (half + 1) * 1024)
            P.act(lambda e, pg=pg, cols=cols: e.activation(out=sg[:, cols], in_=pg, func=AF.Sigmoid),
                  reads=ps_keys(4 * s_ + 2, 2), writes=[("csg", half)])
            P.dve(lambda e, pa=pa, cols=cols, half=half, ui=ui: e.tensor_tensor(
                out=u[ui][:, 30 + half * 1024: 30 + (half + 1) * 1024], in0=pa, in1=sg[:, cols], op=ALU.mult),
                reads=ps_keys(4 * s_, 2) + [("csg", half)], writes=[("cu", ui)])
        P.dve(lambda e, j=j, ui=ui: e.tensor_scalar(out=v[:, j, :], in0=u[ui][:, 0:T], scalar1=cdw[:, j, 0:1], scalar2=cvec[:, j, 0:1],
                                                    op0=ALU.mult, op1=ALU.add),
              reads=[("cu", ui), "cdw", "cvec"], writes=[("cv", j)])
        for t in range(1, 31):
            P.dve(lambda e, j=j, t=t, ui=ui: e.scalar_tensor_tensor(out=v[:, j, :], in0=u[ui][:, t:t + T], scalar=cdw[:, j, t:t + 1],
                                                                    in1=v[:, j, :], op0=ALU.mult, op1=ALU.add),
                  reads=[("cu", ui), "cdw", ("cv", j)], writes=[("cv", j)])
    return cf


def conf_ln(k, l, cf):
    nc, P, ps = k.nc, k.P, k.ps
    v, cvec = cf.v, cf.cvec
    with ExitStack() as st:
        sb = lambda name, shape, dt: st.enter_context(nc.sbuf_tensor(_uniq(name), list(shape), dt))
        vb = sb("cvb", [128, 4, 512], BF16)
        vsq = sb("cvsq", [128, 4, 512], BF16)
        mean = sb("cmean", [128, 512], F32)
        msq = sb("cmsq", [128, 512], F32)
        rstd = sb("crstd", [128, 512], F32)
        tt_ = [sb(f"ctt{i}", [128, 512], F32) for i in range(2)]
        yb = [sb(f"cyb{i}", [128, 512], BF16) for i in range(2)]
        for q in range(4):
            qs = slice(q * 512, (q + 1) * 512)
            for j in range(4):
                P.act(lambda e, j=j, qs=qs: e.activation(out=vb[:, j, :], in_=v[:, j, qs], func=AF.Copy),
                      reads=[("cv", j)], writes=["cvb"])
                P.act(lambda e, j=j, qs=qs: e.activation(out=vsq[:, j, :], in_=v[:, j, qs], func=AF.Square),
                      reads=[("cv", j)], writes=["cvsq"])
            for j in range(4):
                P.pe(lambda e, j=j: e.matmul(ps[:, 0, :], k.ones_bf[:], vb[:, j, :], start=(j == 0), stop=(j == 3)),
                     reads=["cvb", "ones_bf"], writes=ps_keys(0))
            for j in range(4):
                P.pe(lambda e, j=j: e.matmul(ps[:, 1, :], k.ones_bf[:], vsq[:, j, :], start=(j == 0), stop=(j == 3)),
                     reads=["cvsq", "ones_bf"], writes=ps_keys(1))
            P.dve(lambda e: e.tensor_scalar(out=mean[:], in0=ps[:, 0, :], scalar1=1.0 / 512, scalar2=None, op0=ALU.mult),
                  reads=ps_keys(0), writes=["cmean"])
            P.dve(lambda e: e.tensor_tensor(out=msq[:], in0=mean[:], in1=mean[:], op=ALU.mult),
                  reads=["cmean"], writes=["cmsq"])
            P.dve(lambda e: e.scalar_tensor_tensor(out=msq[:], in0=ps[:, 1, :], scalar=1.0 / 512, in1=msq[:],
                                                   op0=ALU.mult, op1=ALU.subtract),
                  reads=ps_keys(1) + ["cmsq"], writes=["cmsq"])
            P.act(lambda e: e.activation(out=rstd[:], in_=msq[:], func=AF.Sqrt, bias=EPS, scale=1.0),
                  reads=["cmsq"], writes=["crstd"])
            P.dve(lambda e: e.reciprocal(out=rstd[:], in_=rstd[:]), reads=["crstd"], writes=["crstd"])
            for j in range(4):
                i = j % 2
                P.dve(lambda e, j=j, i=i, qs=qs: e.tensor_tensor(out=tt_[i][:], in0=v[:, j, qs], in1=mean[:], op=ALU.subtract),
                      reads=[("cv", j), "cmean"], writes=[("ctt", i)])
                P.dve(lambda e, i=i: e.tensor_tensor(out=tt_[i][:], in0=tt_[i][:], in1=rstd[:], op=ALU.mult),
                      reads=[("ctt", i), "crstd"], writes=[("ctt", i)])
                P.act(lambda e, j=j, i=i: e.activation(out=yb[i][:], in_=tt_[i][:], func=AF.Silu,
                                                       scale=cvec[:, j, 1:2], bias=cvec[:, j, 2:3]),
                      reads=[("ctt", i), "cvec"], writes=[("cyb", i)])
                P.dma("sp", lambda e, j=j, i=i, qs=qs: e.dma_start(out=k.yT[512 + j * 128: 512 + (j + 1) * 128, qs], in_=yb[i][:]),
                      reads=[("cyb", i)], writes=[("yT", 4 + j, q)])


def make_in_maps(inp, batches, LW=L):
    f = lambda a: np.ascontiguousarray(np.asarray(a, dtype=np.float32))
    shared = {}
    shared["w_in"] = f(np.asarray(inp["w_in"])[:LW][:, :, PERM])
    shared["w_out"] = f(np.asarray(inp["w_out"])[:LW])
    shared["w_up"] = f(np.asarray(inp["w_up"])[:LW])
    shared["w_down"] = f(np.asarray(inp["w_down"])[:LW])
    shared["nw1"] = f(np.asarray(inp["attn_norm_w"]).reshape(L, 16, 128).transpose(0, 2, 1))
    shared["nw2"] = f(np.asarray(inp["ffn_norm_w"]).reshape(L, 16, 128).transpose(0, 2, 1))
    shared["nwf"] = f(np.asarray(inp["final_norm_w"]).reshape(16, 128).T)
    shared["gbias"] = f(np.concatenate([np.asarray(inp["mlstm_i_bias"]), np.asarray(inp["mlstm_f_bias"])], axis=1))
    shared["mnw"] = f(inp["mlstm_norm_w"])
    shared["cdw"] = f(np.asarray(inp["conv_dw_w"]).transpose(0, 2, 1).reshape(L, 4, 128, 31))
    shared["cvec"] = f(np.stack([np.asarray(inp["conv_dw_b"]), np.asarray(inp["conv_ln_w"]),
                                 np.asarray(inp["conv_ln_b"])], axis=-1).reshape(L, 4, 128, 3))
    shared["peT"] = f(np.stack([np.asarray(inp["nsa_cmp_pe_k"]), np.asarray(inp["nsa_cmp_pe_v"])], axis=1).transpose(0, 1, 3, 2))
    shared["cmpw"] = f(np.stack([np.asarray(inp["nsa_cmp_w_k"]), np.asarray(inp["nsa_cmp_w_v"])], axis=1))
    shared["fdw"] = f(np.asarray(inp["ffn_dw_w"]).reshape(L, 3, 88, 128).transpose(0, 3, 2, 1))
    for n, v in host_consts().items():
        shared["c_" + n] = np.ascontiguousarray(v)
    maps = []
    for b in batches:
        m = dict(shared)
        m["xT"] = f(np.asarray(inp["x"])[b].T)
        maps.append(m)
    return maps


def stage_mlstm(k, l):
    nc, P, ps = k.nc, k.P, k.ps
    SK = 128 ** -0.5
    with ExitStack() as st:
        sb = lambda name, shape, dt: st.enter_context(nc.sbuf_tensor(_uniq(name), list(shape), dt))
        qT = sb("m_qT", [128, 4, T], BF16)
        kT = sb("m_kT", [128, 4, T], BF16)
        ktm = sb("m_ktm", [128, 16, 512], BF16)
        vtm = sb("m_vtm", [128, 16, 512], BF16)
        otm = sb("m_otm", [128, 16, 512], BF16)
        vext = sb("m_vext", [128, 16, 4, 132], BF16)
        g = sb("m_g", [128, 16, 32], F32)
        gb = sb("m_gb", [128, 8], F32)
        mnw = sb("m_mnw", [128, 512], F32)
        tri01 = sb("m_tri01", [128, 512], F32)
        triU = sb("m_triU", [128, 128], F32)
        ii = sb("m_ii", [128, 16, 4], F32)
        ff = sb("m_ff", [128, 16, 4], F32)
        lf = sb("m_lf", [128, 64], F32)
        aa = sb("m_a", [128, 64], F32)
        ea = sb("m_ea", [128, 64], F32)
        ebs = sb("m_ebs", [128, 64], F32)
        ebl = sb("m_ebl", [128, 64], F32)
        Cf = sb("m_Cf", [128, 4, 132], F32)
        tmpC = sb("m_tmpC", [128, 4, 132], F32)
        Cb = sb("m_Cb", [128, 4, 132], BF16)
        Pm = [sb(f"m_Pm{i}", [128, 512], BF16) for i in range(2)]
        sm = [sb(f"m_sm{i}", [128, 8, 16], F32) for i in range(2)]
        hh = [sb(f"m_hh{i}", [128, 16, 128], F32) for i in range(2)]
        sq = sb("m_sq", [128, 16, 128], F32)
        sgo = sb("m_sgo", [128, 4, 512], F32)
        ya = sb("m_ya", [128, 4, 512], BF16)
        raw = sb("m_raw", [128, 16, 4, 132], F32)
        yaT = [sb(f"m_yaT{i}", [128, 512], BF16) for i in range(2)]
        ld = lambda q, out, in_, key, rd=(): P.dma(q, lambda e: e.dma_start(out=out, in_=in_), reads=list(rd), writes=[key])
        ld("sp", qT[:], k.mqT.rearrange("(h d) t -> d h t", d=128), "m_qT")
        ld("sp", kT[:], k.mkT.rearrange("(h d) t -> d h t", d=128), "m_kT")
        ld("sp", ktm[:], k.mk_tm.rearrange("(c p) n -> p c n", p=128), "m_ktm")
        ld("sp", vtm[:], k.mv_tm.rearrange("(c p) n -> p c n", p=128), "m_vtm")
        ld("sp", otm[:], k.mo_tm.rearrange("(c p) n -> p c n", p=128), "m_otm")
        ld("sp", g[:], k.gates.rearrange("(c p) n -> p c n", p=128), "m_g")
        ld("sp", gb[:], k.gbias[l].partition_broadcast(128), "m_gb")
        ld("sp", mnw[:], k.mnw[l].partition_broadcast(128), "m_mnw")
        ld("sp", tri01[:], k.consts["tri01"], "m_tri01")
        ld("sp", triU[:], k.consts["triU"], "m_triU")
        bc = lambda ap, shape: ap.unsqueeze(len(ap.shape)).to_broadcast(shape)
        P.dve(lambda e: e.tensor_tensor(out=ii[:], in0=g[:, :, 0:4], in1=gb[:, 0:4].unsqueeze(1).to_broadcast([128, 16, 4]), op=ALU.add),
              reads=["m_g", "m_gb"], writes=["m_ii"])
        P.dve(lambda e: e.tensor_tensor(out=ff[:], in0=g[:, :, 4:8], in1=gb[:, 4:8].unsqueeze(1).to_broadcast([128, 16, 4]), op=ALU.add),
              reads=["m_g", "m_gb"], writes=["m_ff"])
        ffv = ff[:].rearrange("p c h -> p (c h)")
        iiv = ii[:].rearrange("p c h -> p (c h)")
        P.act(lambda e: e.activation(out=lf[:], in_=ffv, func=AF.Exp, scale=-1.0), reads=["m_ff"], writes=["m_lf"])
        P.act(lambda e: e.activation(out=lf[:], in_=lf[:], func=AF.Ln, bias=1.0, scale=1.0), reads=["m_lf"], writes=["m_lf"])
        P.dve(lambda e: e.tensor_scalar(out=lf[:], in0=lf[:], scalar1=-1.0, scalar2=None, op0=ALU.mult), reads=["m_lf"], writes=["m_lf"])
        P.pe(lambda e: e.matmul(ps[:, 0, 0:64], triU[:], lf[:], start=True, stop=True), reads=["m_triU", "m_lf"], writes=ps_keys(0))
        P.pe(lambda e: e.matmul(ps[:, 1, 0:64], k.ones_f[:], lf[:], start=True, stop=True), reads=["ones_f", "m_lf"], writes=ps_keys(1))
        P.dve(lambda e: e.tensor_tensor(out=aa[:], in0=iiv, in1=ps[:, 0, 0:64], op=ALU.subtract), reads=["m_ii"] + ps_keys(0), writes=["m_a"])
        P.act(lambda e: e.activation(out=ea[:], in_=aa[:], func=AF.Exp), reads=["m_a"], writes=["m_ea"])
        P.act(lambda e: e.activation(out=ebs[:], in_=ps[:, 0, 0:64], func=AF.Exp), reads=ps_keys(0), writes=["m_ebs"])
        P.dve(lambda e: e.tensor_scalar(out=ebs[:], in0=ebs[:], scalar1=SK, scalar2=None, op0=ALU.mult), reads=["m_ebs"], writes=["m_ebs"])
        P.act(lambda e: e.activation(out=ebl[:], in_=ps[:, 1, 0:64], func=AF.Exp), reads=ps_keys(1), writes=["m_ebl"])
        vx = vext[:].rearrange("p c h x -> p (c h) x")
        P.dve(lambda e: e.tensor_tensor(out=vx[:, :, 0:128], in0=vtm[:].rearrange("p c (h x) -> p (c h) x", x=128),
                                        in1=bc(ea[:], [128, 64, 128]), op=ALU.mult),
              reads=["m_vtm", "m_ea"], writes=["m_vext"])
        P.dve(lambda e: e.tensor_copy(out=vx[:, :, 128:129], in_=ea[:].unsqueeze(2)), reads=["m_ea"], writes=["m_vext"])
        P.dve(lambda e: e.memset(Cf[:], 0.0), writes=["m_Cf"])
        for c in range(16):
            cs = slice(c * 128, (c + 1) * 128)
            par = c % 2
            bO = 2 + 2 * par
            for h in range(4):
                P.pe(lambda e, h=h, cs=cs: e.matmul(ps[:, 0, h * 128:(h + 1) * 128], kT[:, h, cs], qT[:, h, cs], start=True, stop=True),
                     reads=["m_kT", "m_qT"], writes=ps_keys(0))
            P.dve(lambda e, par=par: e.tensor_tensor(out=Pm[par][:], in0=ps[:, 0, :], in1=tri01[:], op=ALU.mult),
                  reads=ps_keys(0) + ["m_tri01"], writes=[("m_Pm", par)])
            for h in range(4):
                ov = ps[:, bO + h // 2, (h % 2) * 256:(h % 2) * 256 + 129]
                P.pe(lambda e, h=h, ov=ov, par=par, c=c: e.matmul(ov, Pm[par][:, h * 128:(h + 1) * 128], vext[:, c, h, 0:129],
                                                                  start=True, stop=(c == 0)),
                     reads=[("m_Pm", par), "m_vext"], writes=ps_keys(bO + h // 2))
                if c > 0:
                    P.pe(lambda e, h=h, ov=ov, cs=cs: e.matmul(ov, qT[:, h, cs], Cb[:, h, 0:129], start=False, stop=True),
                         reads=["m_qT", "m_Cb"], writes=ps_keys(bO + h // 2))
            if c < 15:
                for h in range(4):
                    kvv = ps[:, 6 + h // 2, (h % 2) * 256:(h % 2) * 256 + 129]
                    P.pe(lambda e, h=h, kvv=kvv, c=c: e.matmul(kvv, ktm[:, c, h * 128:(h + 1) * 128], vext[:, c, h, 0:129],
                                                               start=True, stop=True),
                         reads=["m_ktm", "m_vext"], writes=ps_keys(6 + h // 2))
                kv4 = ps[:, 6:8, :].rearrange("p a (b x) -> p (a b) x", x=256)[:, :, 0:129]
                P.dve(lambda e, kv4=kv4: e.tensor_tensor(out=tmpC[:, :, 0:129], in0=Cf[:, :, 0:129], in1=kv4, op=ALU.add),
                      reads=["m_Cf"] + ps_keys(6, 2), writes=["m_tmpC"])
                P.dve(lambda e, c=c: e.tensor_tensor(out=Cf[:, :, 0:129], in0=tmpC[:, :, 0:129],
                                                     in1=bc(ebl[:, c * 4:(c + 1) * 4], [128, 4, 129]), op=ALU.mult),
                      reads=["m_tmpC", "m_ebl"], writes=["m_Cf"])
                P.act(lambda e: e.activation(out=Cb[:, :, 0:129], in_=Cf[:, :, 0:129], func=AF.Copy), reads=["m_Cf"], writes=["m_Cb"])
            o4 = ps[:, bO:bO + 2, :].rearrange("p a (b x) -> p (a b) x", x=256)
            P.act(lambda e, o4=o4, c=c: e.activation(out=raw[:, c, :, 0:129], in_=o4[:, :, 0:129], func=AF.Copy),
                  reads=ps_keys(bO, 2), writes=[("m_raw", c // 4)])
            if c % 4 == 3:
                bi = c // 4
                par = bi % 2
                cs4 = slice(bi * 512, (bi + 1) * 512)
                ch = slice(bi * 16, (bi + 1) * 16)
                rw = raw[:, bi * 4:(bi + 1) * 4, :, :].rearrange("p c h x -> p (c h) x")
                s_ = sm[par]
                smk = [("m_sm", par)]
                rk = [("m_raw", bi)]
                H = hh[par]
                hk = [("m_hh", par)]
                ebc = ebs[:, ch]
                P.dve(lambda e, rw=rw, s_=s_, ebc=ebc: e.tensor_tensor(out=s_[:, 0, :], in0=rw[:, :, 128], in1=ebc, op=ALU.mult),
                      reads=rk + ["m_ebs"], writes=smk)
                P.dve(lambda e, s_=s_: e.tensor_scalar(out=s_[:, 1, :], in0=s_[:, 0, :], scalar1=-1.0, scalar2=None, op0=ALU.mult),
                      reads=smk, writes=smk)
                P.dve(lambda e, s_=s_: e.tensor_tensor(out=s_[:, 1, :], in0=s_[:, 1, :], in1=s_[:, 0, :], op=ALU.max),
                      reads=smk, writes=smk)
                P.dve(lambda e, s_=s_: e.tensor_scalar(out=s_[:, 1, :], in0=s_[:, 1, :], scalar1=1.0, scalar2=None, op0=ALU.max),
                      reads=smk, writes=smk)
                P.dve(lambda e, s_=s_: e.reciprocal(out=s_[:, 1, :], in_=s_[:, 1, :]), reads=smk, writes=smk)
                P.dve(lambda e, s_=s_, ebc=ebc: e.tensor_tensor(out=s_[:, 2, :], in0=s_[:, 1, :], in1=ebc, op=ALU.mult),
                      reads=smk + ["m_ebs"], writes=smk)
                P.dve(lambda e, rw=rw, s_=s_, H=H: e.tensor_tensor(out=H[:], in0=rw[:, :, 0:128], in1=bc(s_[:, 2, :], [128, 16, 128]), op=ALU.mult),
                      reads=rk + smk, writes=hk)
                P.dve(lambda e, s_=s_, H=H: e.tensor_reduce(out=s_[:, 3, :], in_=H[:], axis=AX.X, op=ALU.add),
                      reads=hk, writes=smk)
                P.act(lambda e, H=H: e.activation(out=sq[:], in_=H[:], func=AF.Square), reads=hk, writes=["m_sq"])
                P.dve(lambda e, s_=s_: e.tensor_reduce(out=s_[:, 4, :], in_=sq[:], axis=AX.X, op=ALU.add),
                      reads=["m_sq"], writes=smk)
                P.dve(lambda e, s_=s_: e.tensor_scalar(out=s_[:, 3, :], in0=s_[:, 3, :], scalar1=1.0 / 128, scalar2=None, op0=ALU.mult),
                      reads=smk, writes=smk)
                P.dve(lambda e, s_=s_: e.tensor_tensor(out=s_[:, 5, :], in0=s_[:, 3, :], in1=s_[:, 3, :], op=ALU.mult),
                      reads=smk, writes=smk)
                P.dve(lambda e, s_=s_: e.scalar_tensor_tensor(out=s_[:, 5, :], in0=s_[:, 4, :], scalar=1.0 / 128, in1=s_[:, 5, :],
                                                              op0=ALU.mult, op1=ALU.subtract),
                      reads=smk, writes=smk)
                P.act(lambda e, s_=s_: e.activation(out=s_[:, 6, :], in_=s_[:, 5, :], func=AF.Sqrt, bias=EPS, scale=1.0),
                      reads=smk, writes=smk)
                P.dve(lambda e, s_=s_: e.reciprocal(out=s_[:, 6, :], in_=s_[:, 6, :]), reads=smk, writes=smk)
                P.dve(lambda e, s_=s_, H=H: e.tensor_tensor(out=H[:], in0=H[:], in1=bc(s_[:, 3, :], [128, 16, 128]), op=ALU.subtract),
                      reads=hk + smk, writes=hk)
                P.dve(lambda e, s_=s_, H=H: e.tensor_tensor(out=H[:], in0=H[:], in1=bc(s_[:, 6, :], [128, 16, 128]), op=ALU.mult),
                      reads=hk + smk, writes=hk)
                H4 = H[:].rearrange("p (c h) x -> p c (h x)", h=4)
                P.dve(lambda e, H4=H4: e.tensor_tensor(out=H4, in0=H4, in1=mnw[:].unsqueeze(1).to_broadcast([128, 4, 512]), op=ALU.mult),
                      reads=hk + ["m_mnw"], writes=hk)
                P.act(lambda e, bi=bi: e.activation(out=sgo[:], in_=otm[:, bi * 4:(bi + 1) * 4, :], func=AF.Sigmoid),
                      reads=["m_otm"], writes=["m_sgo"])
                P.dve(lambda e, H4=H4: e.tensor_tensor(out=ya[:], in0=H4, in1=sgo[:], op=ALU.mult),
                      reads=hk + ["m_sgo"], writes=["m_ya"])
                for cc in range(4):
                    pT = ps[:, 1, :].bitcast(BF16)
                    for h in range(4):
                        P.pe(lambda e, h=h, pT=pT, cc=cc: e.transpose(pT[:, h * 128:(h + 1) * 128], ya[:, cc, h * 128:(h + 1) * 128], k.ident[:]),
                             reads=["m_ya", "ident"], writes=ps_keys(1))
                    yi = cc % 2
                    P.act(lambda e, pT=pT, yi=yi: e.activation(out=yaT[yi][:], in_=pT[:, 0:512], func=AF.Copy),
                          reads=ps_keys(1), writes=[("m_yaT", yi)])
                    tcs = slice((bi * 4 + cc) * 128, (bi * 4 + cc + 1) * 128)
                    P.dma("sp", lambda e, yi=yi, tcs=tcs: e.dma_start(out=k.yT[0:512, tcs].rearrange("(h d) t -> d h t", d=128),
                                                                      in_=yaT[yi][:].rearrange("p (h t) -> p h t", t=128)),
                          reads=[("m_yaT", yi)], writes=[("yT", "a", bi * 4 + cc)])
    P.barrier()


def stage_wout(k, l, x_src, xkey):
    nc, P, ps = k.nc, k.P, k.ps
    wv = k.w_out[l].rearrange("(kc p) n -> p kc n", p=128)
    with ExitStack() as st:
        sb = lambda name, shape, dt: st.enter_context(nc.sbuf_tensor(_uniq(name), list(shape), dt))
        yTs = sb("o_yT", [128, 16, T], BF16)
        wb = [sb(f"o_wb{i}", [128, 16, 512], BF16) for i in range(2)]
        xold = [sb(f"o_xold{i}", [128, 1024], F32) for i in range(3)]
        P.dma("sp", lambda e: e.dma_start(out=yTs[:], in_=k.yT.rearrange("(kc p) t -> p kc t", p=128)),
              reads=[("yT", "all")], writes=["o_yT"])
        n = 0
        for gi in range(4):
            P.dma("pool", lambda e, gi=gi: e.dma_start(out=wb[gi % 2][:], in_=wv[:, :, gi * 512:(gi + 1) * 512]),
                  writes=[("o_wb", gi % 2)])
            for c in range(4):
                dc = gi * 4 + c
                for half in range(2):
                    b0 = (n % 4) * 2
                    xi = n % 3
                    n += 1
                    cols = slice(half * 1024, (half + 1) * 1024)
                    P.dma("sp", lambda e, xi=xi, dc=dc, cols=cols: e.dma_start(out=xold[xi][:], in_=x_src[dc * 128:(dc + 1) * 128, cols]),
                          reads=[(xkey, dc)], writes=[("o_xold", xi)])
                    for kc in range(16):
                        for q in range(2):
                            P.pe(lambda e, kc=kc, q=q, b0=b0, gi=gi, c=c, half=half: e.matmul(
                                ps[:, b0 + q, :], wb[gi % 2][:, kc, c * 128:(c + 1) * 128],
                                yTs[:, kc, half * 1024 + q * 512: half * 1024 + (q + 1) * 512],
                                start=(kc == 0), stop=(kc == 15)),
                                reads=[("o_wb", gi % 2), "o_yT"], writes=[("ps", b0 + q)])
                    psv = ps[:, b0:b0 + 2, :].rearrange("p a b -> p (a b)")
                    P.dve(lambda e, psv=psv, xi=xi: e.tensor_tensor(out=xold[xi][:], in0=psv, in1=xold[xi][:], op=ALU.add),
                          reads=ps_keys(b0, 2) + [("o_xold", xi)], writes=[("o_xold", xi)])
                    P.dma("sp", lambda e, xi=xi, dc=dc, cols=cols: e.dma_start(out=k.xm[dc * 128:(dc + 1) * 128, cols], in_=xold[xi][:]),
                          reads=[("o_xold", xi)], writes=[("xm", dc)])
    P.barrier()


def stage_ffn(k, l):
    nc, P, ps = k.nc, k.P, k.ps
    wu = k.w_up[l].rearrange("(kc p) n -> p kc n", p=128)
    wd = k.w_down[l].rearrange("(fc p) n -> p fc n", p=128)
    with ExitStack() as st:
        sb = lambda name, shape, dt: st.enter_context(nc.sbuf_tensor(_uniq(name), list(shape), dt))
        hT = sb("f_hT", [128, 16, T], BF16)
        with ExitStack() as st2:
            rms_to_hT(k, st2, l, k.xm, "xm", k.nw2[l], hT)
        P.barrier()
        with ExitStack() as st3:
            sb3 = lambda name, shape, dt: st3.enter_context(nc.sbuf_tensor(_uniq(name), list(shape), dt))
            wg = [sb3(f"f_wg{i}", [128, 16, 512], BF16) for i in range(2)]
            wvv = [sb3(f"f_wv{i}", [128, 16, 512], BF16) for i in range(2)]
            fdw = sb3("f_fdw", [128, 88, 3], F32)
            ug = [sb3(f"f_ug{i}", [128, T + 2], F32) for i in range(2)]
            uv = [sb3(f"f_uv{i}", [128, T + 2], F32) for i in range(2)]
            cg = sb3("f_cg", [128, T], F32)
            cv = sb3("f_cv", [128, T], F32)
            ab = [sb3(f"f_ab{i}", [128, T], BF16) for i in range(2)]
            P.dma("sp", lambda e: e.dma_start(out=fdw[:], in_=k.fdw[l]), writes=["f_fdw"])
            for i in range(2):
                P.dve(lambda e, i=i: e.memset(ug[i][:, 0:2], 0.0), writes=[("f_ug", i)])
                P.dve(lambda e, i=i: e.memset(uv[i][:, 0:2], 0.0), writes=[("f_uv", i)])
            n = 0
            for gi in range(11):
                P.dma("pool", lambda e, gi=gi: e.dma_start(out=wg[gi % 2][:], in_=wu[:, :, gi * 512:(gi + 1) * 512]),
                      writes=[("f_wg", gi % 2)])
                P.dma("pool", lambda e, gi=gi: e.dma_start(out=wvv[gi % 2][:], in_=wu[:, :, DFF + gi * 512:DFF + (gi + 1) * 512]),
                      writes=[("f_wv", gi % 2)])
                for c4 in range(4):
                    c = gi * 4 + c4
                    ui = c % 2
                    for half in range(2):
                        s_ = n % 2
                        n += 1
                        for (wt, wkey, b0, dst, dkey) in ((wg, "f_wg", 4 * s_, ug, "f_ug"), (wvv, "f_wv", 4 * s_ + 2, uv, "f_uv")):
                            for kc in range(16):
                                for q in range(2):
                                    P.pe(lambda e, kc=kc, q=q, b0=b0, wt=wt, gi=gi, c4=c4, half=half: e.matmul(
                                        ps[:, b0 + q, :], wt[gi % 2][:, kc, c4 * 128:(c4 + 1) * 128],
                                        hT[:, kc, half * 1024 + q * 512: half * 1024 + (q + 1) * 512],
                                        start=(kc == 0), stop=(kc == 15)),
                                        reads=[(wkey, gi % 2), "hT"], writes=[("ps", b0 + q)])
                            psv = ps[:, b0:b0 + 2, :].rearrange("p a b -> p (a b)")
                            P.act(lambda e, psv=psv, dst=dst, ui=ui, half=half: e.activation(
                                out=dst[ui][:, 2 + half * 1024: 2 + (half + 1) * 1024], in_=psv, func=AF.Copy),
                                reads=ps_keys(b0, 2), writes=[(dkey, ui)])
                    P.dve(lambda e, c=c, ui=ui: e.tensor_scalar(out=cg[:], in0=ug[ui][:, 2:2 + T], scalar1=fdw[:, c, 2:3], scalar2=None, op0=ALU.mult),
                          reads=[("f_ug", ui), "f_fdw"], writes=["f_cg"])
                    for t in (1, 0):
                        P.dve(lambda e, c=c, ui=ui, t=t: e.scalar_tensor_tensor(out=cg[:], in0=ug[ui][:, t:t + T], scalar=fdw[:, c, t:t + 1],
                                                                                in1=cg[:], op0=ALU.mult, op1=ALU.add),
                              reads=[("f_ug", ui), "f_fdw", "f_cg"], writes=["f_cg"])
                    P.dve(lambda e, c=c, ui=ui: e.tensor_scalar(out=cv[:], in0=uv[ui][:, 2:2 + T], scalar1=fdw[:, 44 + c, 2:3], scalar2=None, op0=ALU.mult),
                           reads=[("f_uv", ui), "f_fdw"], writes=["f_cv"])
                    for t in (1, 0):
                        P.dve(lambda e, c=c, ui=ui, t=t: e.scalar_tensor_tensor(out=cv[:], in0=uv[ui][:, t:t + T], scalar=fdw[:, 44 + c, t:t + 1],
                                                                                 in1=cv[:], op0=ALU.mult, op1=ALU.add),
                               reads=[("f_uv", ui), "f_fdw", "f_cv"], writes=["f_cv"])
                    P.act(lambda e: e.activation(out=cg[:], in_=cg[:], func=AF.Silu), reads=["f_cg"], writes=["f_cg"])
                    P.dve(lambda e, ui=ui: e.tensor_tensor(out=ab[ui][:], in0=cg[:], in1=cv[:], op=ALU.mult),
                          reads=["f_cg", "f_cv"], writes=[("f_ab", ui)])
                    P.dma("sp", lambda e, ui=ui, c=c: e.dma_start(out=k.aT[c * 128:(c + 1) * 128, :], in_=ab[ui][:]),
                          reads=[("f_ab", ui)], writes=[("aT", c)])
        P.barrier()
    with ExitStack() as st:
        sb = lambda name, shape, dt: st.enter_context(nc.sbuf_tensor(_uniq(name), list(shape), dt))
        aTs = sb("d_aT", [128, 44, 1024], BF16)
        wdb = [sb(f"d_wd{i}", [128, 44, 128], BF16) for i in range(3)]
        xold = [sb(f"d_xold{i}", [128, 1024], F32) for i in range(3)]
        n = 0
        for half in range(2):
            cols = slice(half * 1024, (half + 1) * 1024)
            P.dma("sp", lambda e, cols=cols: e.dma_start(out=aTs[:], in_=k.aT.rearrange("(fc p) t -> p fc t", p=128)[:, :, cols]),
                  reads=[("aT", "all")], writes=["d_aT"])
            for dc in range(16):
                wi = n % 3
                b0 = (n % 4) * 2
                n += 1
                P.dma("pool", lambda e, wi=wi, dc=dc: e.dma_start(out=wdb[wi][:], in_=wd[:, :, dc * 128:(dc + 1) * 128]),
                      writes=[("d_wd", wi)])
                P.dma("sp", lambda e, wi=wi, dc=dc, cols=cols: e.dma_start(out=xold[wi][:], in_=k.xm[dc * 128:(dc + 1) * 128, cols]),
                      reads=[("xm", dc)], writes=[("d_xold", wi)])
                for fc in range(44):
                    for q in range(2):
                        P.pe(lambda e, fc=fc, q=q, b0=b0, wi=wi: e.matmul(
                            ps[:, b0 + q, :], wdb[wi][:, fc, :], aTs[:, fc, q * 512:(q + 1) * 512],
                            start=(fc == 0), stop=(fc == 43)),
                            reads=[("d_wd", wi), "d_aT"], writes=[("ps", b0 + q)])
                psv = ps[:, b0:b0 + 2, :].rearrange("p a b -> p (a b)")
                P.dve(lambda e, psv=psv, wi=wi: e.tensor_tensor(out=xold[wi][:], in0=psv, in1=xold[wi][:], op=ALU.add),
                      reads=ps_keys(b0, 2) + [("d_xold", wi)], writes=[("d_xold", wi)])
                P.dma("sp", lambda e, wi=wi, dc=dc, cols=cols: e.dma_start(out=k.xo[dc * 128:(dc + 1) * 128, cols], in_=xold[wi][:]),
                      reads=[("d_xold", wi)], writes=[("xo", dc)])
    P.barrier()


def stage_final(k):
    nc, P, ps = k.nc, k.P, k.ps
    with ExitStack() as st:
        sb = lambda name, shape, dt: st.enter_context(nc.sbuf_tensor(_uniq(name), list(shape), dt))
        xin = [sb(f"fn_xin{i}", [128, T], F32) for i in range(2)]
        sq = [sb(f"fn_sq{i}", [128, T], BF16) for i in range(2)]
        rstd = sb("fn_rstd", [128, T], F32)
        nw = sb("fn_nw", [128, 16], F32)
        P.dma("sp", lambda e: e.dma_start(out=nw[:], in_=k.nwf), writes=["fn_nw"])
        for kc in range(16):
            i = kc % 2
            P.dma("sp", lambda e, kc=kc, i=i: e.dma_start(out=xin[i][:], in_=k.xo[kc * 128:(kc + 1) * 128, :]),
                  reads=[("xo", kc)], writes=[("fn_xin", i)])
            P.act(lambda e, i=i: e.activation(out=sq[i][:], in_=xin[i][:], func=AF.Square),
                  reads=[("fn_xin", i)], writes=[("fn_sq", i)])
            for q in range(4):
                P.pe(lambda e, kc=kc, i=i, q=q: e.matmul(ps[:, q, :], k.ones_bf[:], sq[i][:, q * 512:(q + 1) * 512],
                                                          start=(kc == 0), stop=(kc == 15)),
                     reads=[("fn_sq", i), "ones_bf"], writes=ps_keys(q))
        for q in range(4):
            P.act(lambda e, q=q: e.activation(out=rstd[:, q * 512:(q + 1) * 512], in_=ps[:, q, :], func=AF.Sqrt,
                                              bias=EPS, scale=1.0 / D),
                  reads=ps_keys(q), writes=["fn_rstd"])
        P.dve(lambda e: e.reciprocal(out=rstd[:], in_=rstd[:]), reads=["fn_rstd"], writes=["fn_rstd"])
        for kc in range(16):
            i = kc % 2
            P.dma("sp", lambda e, kc=kc, i=i: e.dma_start(out=xin[i][:], in_=k.xo[kc * 128:(kc + 1) * 128, :]),
                  reads=[("xo", kc)], writes=[("fn_xin", i)])
            P.dve(lambda e, kc=kc, i=i: e.scalar_tensor_tensor(out=xin[i][:], in0=xin[i][:], scalar=nw[:, kc:kc + 1],
                                                               in1=rstd[:], op0=ALU.mult, op1=ALU.mult),
                  reads=[("fn_xin", i), "fn_rstd", "fn_nw"], writes=[("fn_xin", i)])
            P.dma("sp", lambda e, kc=kc, i=i: e.dma_start(out=k.outT[kc * 128:(kc + 1) * 128, :], in_=xin[i][:]),
                  reads=[("fn_xin", i)], writes=[("outT", kc)])


def kernel(**inputs):
    nc, k = build(n_layers=L, stages=("s1", "conf", "mlstm", "nsa", "wout", "ffn", "final"))
    maps = make_in_maps(inputs, [0, 1, 2, 3])
    res = run_bass_kernel_spmd(nc, maps, core_ids=[0, 1, 2, 3])
    out = np.stack([np.asarray(r["outT"], dtype=np.float32).T for r in res.results], axis=0)
    return np.ascontiguousarray(out)


def stage_nsa(k, l):
    nc, P, ps = k.nc, k.P, k.ps
    SC = 128 ** -0.5
    with ExitStack() as st:
        sb = lambda name, shape, dt: st.enter_context(nc.sbuf_tensor(_uniq(name), list(shape), dt))
        qT = sb("n_qT", [128, 16, 512], BF16)
        kcr = sb("n_kcr", [128, T], BF16)
        vcr = sb("n_vcr", [128, T], BF16)
        ksT = sb("n_ksT", [128, T], BF16)
        kwT = sb("n_kwT", [128, T], BF16)
        vs = sb("n_vs", [128, 16, 132], BF16)
        vw = sb("n_vw", [128, 16, 132], BF16)
        wk = sb("n_wk", [128, 32, 128], BF16)
        wv = sb("n_wv", [128, 32, 128], BF16)
        peb = sb("n_peb", [128, 2, 32], BF16)
        cmpneg = sb("n_cmpneg", [128, T], BF16)
        ovl = sb("n_ovl", [128, 32], BF16)
        expand = sb("n_expand", [32, T], BF16)
        trineg = sb("n_trineg", [128, 512], BF16)
        farneg = sb("n_farneg", [128, 512], BF16)
        selvm = sb("n_selvm", [128, 16, 32], F32)
        selam = sb("n_selam", [128, 16, 32], F32)
        cosC = sb("n_cosC", [128, 128], F32)
        sinC = sb("n_sinC", [128, 128], F32)
        sg = sb("n_sg", [128, 16, 24], F32)
        bk = sb("n_bk", [128, 1], F32)
        kcf = sb("n_kcf", [128, 128], F32)
        kcb = sb("n_kcb", [128, 128], BF16)
        kt1 = sb("n_kt1", [128, 128], F32)
        kcT = sb("n_kcT", [128, 128], BF16)
        bvr = sb("n_bvr", [1, 128], BF16)
        vce = sb("n_vce", [128, 164], BF16)
        PT = [sb(f"n_PT{i}", [128, 17, 512], BF16) for i in range(2)]
        PTw = [sb(f"n_PTw{i}", [128, 5, 512], BF16) for i in range(2)]
        PTc = [sb(f"n_PTc{i}", [128, 512], BF16) for i in range(2)]
        sm = [sb(f"n_sm{i}", [128, 8, 4], F32) for i in range(2)]
        imp = sb("n_imp", [128, 32], F32)
        sc = sb("n_sc", [128, 32], F32)
        sc2 = sb("n_sc2", [128, 32], F32)
        m8 = sb("n_m8", [128, 16], F32)
        negm = sb("n_negm", [128, 32], BF16)
        selT4 = [sb(f"n_selT{i}", [32, 4, 128], BF16) for i in range(2)]
        ycomb = [sb(f"n_yc{i}", [128, 4, 128], F32) for i in range(2)]
        yb = [sb(f"n_yb{i}", [128, 512], BF16) for i in range(2)]
        ybT = [sb(f"n_ybT{i}", [128, 512], BF16) for i in range(2)]
        ld = lambda q, out, in_, key, rd=(): P.dma(q, lambda e: e.dma_start(out=out, in_=in_), reads=list(rd), writes=[key])
        C = k.consts
        ld("sp", cmpneg[:], C["cmpneg"], "n_cmpneg")
        ld("sp", ovl[:], C["ovl"], "n_ovl")
        ld("sp", expand[:], C["expand"], "n_expand")
        ld("sp", trineg[:], C["trineg"], "n_trineg")
        ld("sp", farneg[:], C["farneg"], "n_farneg")
        ld("sp", selvm[:], C["selvm"], "n_selvm")
        ld("sp", selam[:], C["selam"], "n_selam")
        ld("sp", cosC[:], C["cosC"], "n_cosC")
        ld("sp", sinC[:], C["sinC"], "n_sinC")
        ld("sp", sg[:], k.gates.rearrange("(tb p) n -> p tb n", p=128)[:, :, 8:32], "n_sg")
        P.act(lambda e: e.activation(out=sg[:], in_=sg[:], func=AF.Sigmoid), reads=["n_sg"], writes=["n_sg"])
        ld("pool", wk[:], k.cmpw[l, 0].rearrange("(l d) o -> d l o", d=128), "n_wk")
        ld("pool", wv[:], k.cmpw[l, 1].rearrange("(l d) o -> d l o", d=128), "n_wv")
        ld("pool", peb[:], k.peT[l].rearrange("k d l -> d k l"), "n_peb")
        nS = [0]

        def sbank():
            b = nS[0] % 2
            nS[0] += 1
            return b

        for g in range(2):
            ld("sp", qT[:], k.nqT[g].rearrange("d tb h t -> d tb (h t)"), "n_qT")
            ld("sp", kcr[:], k.ncT[g * 128:(g + 1) * 128, :], "n_kcr")
            ld("sp", vcr[:], k.ncT[256 + g * 128:256 + (g + 1) * 128, :], "n_vcr")
            ld("sp", ksT[:], k.nkT[g * 128:(g + 1) * 128, :], "n_ksT")
            ld("sp", kwT[:], k.nkT[256 + g * 128:256 + (g + 1) * 128, :], "n_kwT")
            nvv = k.nv_tm.rearrange("(kt p) d -> p kt d", p=128)
            ld("sp", vs[:, :, 0:128], nvv[:, :, g * 128:(g + 1) * 128], "n_vs")
            ld("sp", vw[:, :, 0:128], nvv[:, :, 256 + g * 128:256 + (g + 1) * 128], "n_vw")
            P.dve(lambda e: e.memset(vs[:, :, 128:129], 1.0), writes=["n_vs"])
            P.dve(lambda e: e.memset(vw[:, :, 128:129], 1.0), writes=["n_vw"])
            for li in range(32):
                P.pe(lambda e, li=li: e.matmul(ps[:, 0, 0:127], wk[:, li, :], kcr[:, li:li + 16 * 126 + 1:16],
                                               start=(li == 0), stop=(li == 31)),
                     reads=["n_wk", "n_kcr"], writes=ps_keys(0))
            for li in range(32):
                P.pe(lambda e, li=li: e.matmul(ps[:, 1, 0:1], wk[:, li, :], peb[:, 0, li:li + 1],
                                               start=(li == 0), stop=(li == 31)),
                     reads=["n_wk", "n_peb"], writes=ps_keys(1))
            P.act(lambda e: e.activation(out=bk[:], in_=ps[:, 1, 0:1], func=AF.Copy), reads=ps_keys(1), writes=["n_bk"])
            P.dve(lambda e: e.memset(kcf[:], 0.0), writes=["n_kcf"])
            P.act(lambda e: e.activation(out=kcf[:, 0:127], in_=ps[:, 0, 0:127], func=AF.Identity, bias=bk[:, 0:1], scale=1.0),
                  reads=ps_keys(0) + ["n_bk"], writes=["n_kcf"])
            P.act(lambda e: e.activation(out=kcb[:], in_=kcf[:], func=AF.Copy), reads=["n_kcf"], writes=["n_kcb"])
            P.pe(lambda e: e.matmul(ps[:, 2, 0:128], k.pswap[:], kcb[:], start=True, stop=True), reads=["pswap", "n_kcb"], writes=ps_keys(2))
            P.dve(lambda e: e.tensor_tensor(out=kt1[:], in0=ps[:, 2, 0:128], in1=sinC[:], op=ALU.mult), reads=ps_keys(2) + ["n_sinC"], writes=["n_kt1"])
            P.dve(lambda e: e.tensor_tensor(out=kcf[:], in0=kcf[:], in1=cosC[:], op=ALU.mult), reads=["n_kcf", "n_cosC"], writes=["n_kcf"])
            P.dve(lambda e: e.tensor_tensor(out=kcT[:], in0=kcf[:], in1=kt1[:], op=ALU.add), reads=["n_kcf", "n_kt1"], writes=["n_kcT"])
            for li in range(32):
                P.pe(lambda e, li=li: e.matmul(ps[0:1, 4, 0:128], peb[:, 1, li:li + 1], wv[:, li, :],
                                               start=(li == 0), stop=(li == 31)),
                     reads=["n_wv", "n_peb"], writes=ps_keys(4))
            P.act(lambda e: e.activation(out=bvr[:], in_=ps[0:1, 4, 0:128], func=AF.Copy), reads=ps_keys(4), writes=["n_bvr"])
            for li in range(32):
                P.pe(lambda e, li=li: e.matmul(ps[0:127, 3, 0:128], vcr[:, li:li + 16 * 126 + 1:16], wv[:, li, :],
                                               start=(li == 0), stop=False),
                     reads=["n_wv", "n_vcr"], writes=ps_keys(3))
            P.pe(lambda e: e.matmul(ps[0:127, 3, 0:128], k.ones_bf[0:1, 0:127], bvr[:], start=False, stop=True),
                 reads=["ones_bf", "n_bvr"], writes=ps_keys(3))
            P.dve(lambda e: e.memset(vce[:], 0.0), writes=["n_vce"])
            P.act(lambda e: e.activation(out=vce[0:127, 0:128], in_=ps[0:127, 3, 0:128], func=AF.Copy), reads=ps_keys(3), writes=["n_vce"])
            P.dve(lambda e: e.memset(vce[:, 128:129], 1.0), writes=["n_vce"])
            P.dve(lambda e: e.tensor_copy(out=vce[:, 129:161], in_=ovl[:]), reads=["n_ovl"], writes=["n_vce"])
            def ctx(tb):
                par = tb % 2
                return (par, qT[:, tb, :], sm[par], ycomb[par],
                        sg[:, tb, g * 12:(g + 1) * 12].rearrange("p (h r) -> p h r", r=3), [("n_sm", par)])

            def cmp_sel(tb):
                par, qs, s_, Y, sgv, smk = ctx(tb)
                bS = sbank()
                P.pe(lambda e, bS=bS, qs=qs: e.matmul(ps[:, bS, :], kcT[:], qs, start=True, stop=False),
                     reads=["n_kcT", "n_qT"], writes=ps_keys(bS))
                P.pe(lambda e, bS=bS, tb=tb: e.matmul(ps[:, bS, :], k.ident[:],
                                                       cmpneg[:, tb * 128:(tb + 1) * 128].unsqueeze(1).to_broadcast([128, 4, 128]),
                                                       start=False, stop=True),
                     reads=["ident", "n_cmpneg"], writes=ps_keys(bS))
                P.act(lambda e, bS=bS, par=par: e.activation(out=PTc[par][:], in_=ps[:, bS, :], func=AF.Exp, scale=SC),
                      reads=ps_keys(bS), writes=[("n_PTc", par)])
                for h in range(4):
                    P.pe(lambda e, h=h, par=par: e.matmul(ps[:, 2 + h // 2, (h % 2) * 256:(h % 2) * 256 + 161],
                                                          PTc[par][:, h * 128:(h + 1) * 128], vce[:, 0:161], start=True, stop=True),
                         reads=[("n_PTc", par), "n_vce"], writes=ps_keys(2 + h // 2))
                oc = ps[:, 2:4, :].rearrange("p a (b x) -> p (a b) x", x=256)
                rkc = ps_keys(2, 2)
                P.dve(lambda e, oc=oc, s_=s_: e.tensor_scalar(out=s_[:, 0, :], in0=oc[:, :, 128], scalar1=1e-30, scalar2=None, op0=ALU.max),
                      reads=rkc, writes=smk)
                P.dve(lambda e, s_=s_: e.reciprocal(out=s_[:, 0, :], in_=s_[:, 0, :]), reads=smk, writes=smk)
                P.dve(lambda e, oc=oc, s_=s_: e.tensor_scalar(out=imp[:], in0=oc[:, 0, 129:161], scalar1=s_[:, 0, 0:1], scalar2=None, op0=ALU.mult),
                      reads=rkc + smk, writes=["n_imp"])
                for h in range(1, 4):
                    P.dve(lambda e, oc=oc, s_=s_, h=h: e.scalar_tensor_tensor(out=imp[:], in0=oc[:, h, 129:161], scalar=s_[:, 0, h:h + 1],
                                                                              in1=imp[:], op0=ALU.mult, op1=ALU.add),
                          reads=rkc + smk + ["n_imp"], writes=["n_imp"])
                P.dve(lambda e, s_=s_, sgv=sgv: e.tensor_tensor(out=s_[:, 1, :], in0=s_[:, 0, :], in1=sgv[:, :, 0], op=ALU.mult),
                      reads=smk + ["n_sg"], writes=smk)
                P.dve(lambda e, oc=oc, s_=s_, Y=Y: e.tensor_tensor(out=Y[:], in0=oc[:, :, 0:128],
                                                                   in1=s_[:, 1, :].unsqueeze(2).to_broadcast([128, 4, 128]), op=ALU.mult),
                      reads=rkc + smk, writes=[("n_yc", par)])
                P.dve(lambda e, tb=tb: e.tensor_tensor(out=sc[:], in0=imp[:], in1=selvm[:, tb, :], op=ALU.mult),
                      reads=["n_imp", "n_selvm"], writes=["n_sc"])
                P.dve(lambda e, tb=tb: e.tensor_tensor(out=sc[:], in0=sc[:], in1=selam[:, tb, :], op=ALU.add),
                      reads=["n_sc", "n_selam"], writes=["n_sc"])
                P.dve(lambda e: e.max(out=m8[:, 0:8], in_=sc[:]), reads=["n_sc"], writes=["n_m8"])
                P.dve(lambda e: e.match_replace(out=sc2[:], in_to_replace=m8[:, 0:8], in_values=sc[:], imm_value=-3e4),
                      reads=["n_sc", "n_m8"], writes=["n_sc2"])
                P.dve(lambda e: e.max(out=m8[:, 8:16], in_=sc2[:]), reads=["n_sc2"], writes=["n_m8"])
                P.dve(lambda e: e.tensor_scalar(out=sc2[:], in0=sc[:], scalar1=m8[:, 15:16], scalar2=None, op0=ALU.is_lt),
                      reads=["n_sc", "n_m8"], writes=["n_sc2"])
                P.dve(lambda e: e.tensor_scalar(out=negm[:], in0=sc2[:], scalar1=NEGM, scalar2=None, op0=ALU.mult),
                      reads=["n_sc2"], writes=["n_negm"])
                pTs = ps[:, 3, :].bitcast(BF16)[0:32, 896:1024]
                P.pe(lambda e, pTs=pTs: e.transpose(pTs, negm[:], k.ident[:]), reads=["n_negm", "ident"], writes=ps_keys(3))
                P.act(lambda e, pTs=pTs, par=par: e.activation(out=selT4[par][:], in_=pTs.unsqueeze(1).to_broadcast([32, 4, 128]), func=AF.Copy),
                      reads=ps_keys(3), writes=[("n_selT", par)])

            def win_kts(tb):
                return list(range(max(0, tb - 4), tb + 1))

            def s_phase(tb):
                par, qs, s_, Y, sgv, smk = ctx(tb)
                for j, kt in enumerate(win_kts(tb)):
                    bS = sbank()
                    ks = slice(kt * 128, (kt + 1) * 128)
                    extra = []
                    if kt == tb:
                        extra.append((trineg, "n_trineg"))
                    if kt == tb - 4:
                        extra.append((farneg, "n_farneg"))
                    P.pe(lambda e, bS=bS, ks=ks, qs=qs, last=(not extra): e.matmul(ps[:, bS, :], kwT[:, ks], qs, start=True, stop=last),
                         reads=["n_kwT", "n_qT"], writes=ps_keys(bS))
                    for xi, (mt, mk_) in enumerate(extra):
                        P.pe(lambda e, bS=bS, mt=mt, last=(xi == len(extra) - 1): e.matmul(ps[:, bS, :], k.ident[:], mt[:], start=False, stop=last),
                             reads=["ident", mk_], writes=ps_keys(bS))
                    P.act(lambda e, bS=bS, par=par, j=j: e.activation(out=PTw[par][:, j, :], in_=ps[:, bS, :], func=AF.Exp, scale=SC),
                          reads=ps_keys(bS), writes=[("n_PTw", par)])
                selr = selT4[par][:].rearrange("p h t -> p (h t)")
                for kt in range(tb + 1):
                    bS = sbank()
                    ks = slice(kt * 128, (kt + 1) * 128)
                    P.pe(lambda e, bS=bS, ks=ks, qs=qs: e.matmul(ps[:, bS, :], ksT[:, ks], qs, start=True, stop=False),
                         reads=["n_ksT", "n_qT"], writes=ps_keys(bS))
                    P.pe(lambda e, bS=bS, ks=ks, selr=selr, kt=kt, tb=tb: e.matmul(ps[:, bS, :], expand[:, ks], selr, start=False, stop=(kt != tb)),
                         reads=["n_expand", ("n_selT", par)], writes=ps_keys(bS))
                    if kt == tb:
                        P.pe(lambda e, bS=bS: e.matmul(ps[:, bS, :], k.ident[:], trineg[:], start=False, stop=True),
                             reads=["ident", "n_trineg"], writes=ps_keys(bS))
                    P.act(lambda e, bS=bS, par=par, kt=kt: e.activation(out=PT[par][:, kt, :], in_=ps[:, bS, :], func=AF.Exp, scale=SC),
                          reads=ps_keys(bS), writes=[("n_PT", par)])

            def pv_phase(tb):
                par, qs, s_, Y, sgv, smk = ctx(tb)
                kts = win_kts(tb)
                for h in range(4):
                    for j, kt in enumerate(kts):
                        P.pe(lambda e, h=h, kt=kt, j=j, par=par, n=len(kts): e.matmul(
                            ps[:, 6 + h // 2, (h % 2) * 256:(h % 2) * 256 + 129],
                            PTw[par][:, j, h * 128:(h + 1) * 128], vw[:, kt, 0:129], start=(j == 0), stop=(j == n - 1)),
                            reads=[("n_PTw", par), "n_vw"], writes=ps_keys(6 + h // 2))
                for h in range(4):
                    for kt in range(tb + 1):
                        P.pe(lambda e, h=h, kt=kt, par=par, tb=tb: e.matmul(
                            ps[:, 4 + h // 2, (h % 2) * 256:(h % 2) * 256 + 129],
                            PT[par][:, kt, h * 128:(h + 1) * 128], vs[:, kt, 0:129], start=(kt == 0), stop=(kt == tb)),
                            reads=[("n_PT", par), "n_vs"], writes=ps_keys(4 + h // 2))

            def combine(tb):
                par, qs, s_, Y, sgv, smk = ctx(tb)
                for br, b0 in ((2, 6), (1, 4)):
                    ob_ = ps[:, b0:b0 + 2, :].rearrange("p a (b x) -> p (a b) x", x=256)
                    rk = ps_keys(b0, 2)
                    P.dve(lambda e, ob_=ob_, s_=s_, br=br: e.tensor_scalar(out=s_[:, 2 * br, :], in0=ob_[:, :, 128], scalar1=1e-30, scalar2=None, op0=ALU.max),
                          reads=rk, writes=smk)
                    P.dve(lambda e, s_=s_, br=br: e.reciprocal(out=s_[:, 2 * br, :], in_=s_[:, 2 * br, :]), reads=smk, writes=smk)
                    P.dve(lambda e, s_=s_, br=br, sgv=sgv: e.tensor_tensor(out=s_[:, 2 * br + 1, :], in0=s_[:, 2 * br, :], in1=sgv[:, :, br], op=ALU.mult),
                          reads=smk + ["n_sg"], writes=smk)
                    for h in range(4):
                        P.dve(lambda e, ob_=ob_, s_=s_, br=br, h=h, Y=Y: e.scalar_tensor_tensor(
                            out=Y[:, h, :], in0=ob_[:, h, 0:128], scalar=s_[:, 2 * br + 1, h:h + 1], in1=Y[:, h, :],
                            op0=ALU.mult, op1=ALU.add),
                            reads=rk + smk + [("n_yc", par)], writes=[("n_yc", par)])
                P.act(lambda e, Y=Y, par=par: e.activation(out=yb[par][:], in_=Y[:].rearrange("p h x -> p (h x)"), func=AF.Copy),
                      reads=[("n_yc", par)], writes=[("n_yb", par)])

            def finish(tb):
                par = tb % 2
                bT = sbank()
                pT = ps[:, bT, :].bitcast(BF16)
                for h in range(4):
                    P.pe(lambda e, h=h, pT=pT, par=par: e.transpose(pT[:, h * 128:(h + 1) * 128], yb[par][:, h * 128:(h + 1) * 128], k.ident[:]),
                         reads=[("n_yb", par), "ident"], writes=ps_keys(bT))
                P.act(lambda e, pT=pT, par=par: e.activation(out=ybT[par][:], in_=pT[:, 0:512], func=AF.Copy),
                      reads=ps_keys(bT), writes=[("n_ybT", par)])
                r0 = 1024 + g * 512
                P.dma("sp", lambda e, par=par, tb=tb, r0=r0: e.dma_start(
                    out=k.yT[r0:r0 + 512, tb * 128:(tb + 1) * 128].rearrange("(h d) t -> d h t", d=128),
                    in_=ybT[par][:].rearrange("p (h t) -> p h t", t=128)),
                    reads=[("n_ybT", par)], writes=[("yT", "c", g, tb)])

            cmp_sel(0)
            for tb in range(16):
                if tb + 1 < 16:
                    cmp_sel(tb + 1)
                s_phase(tb)
                if tb > 0:
                    finish(tb - 1)
                pv_phase(tb)
                combine(tb)
            finish(15)
    P.barrier()
```
